# Optimizing a Trainium2 kernel written in Bass

```python
import math
import jax, jax.numpy as jnp
from jax import lax
import numpy as np

D_MODEL = 1024
BATCH = 8
SEQ = 2048
DEPTH = 1
DEC_BATCH = 16
DEC_SEQ = 16
PAST_LEN = 4096

CHUNK = 64
Q_BLOCK = 128
MIX_WIDTH = D_MODEL
MLA_HEADS = 8
NOPE_DIM = 64
ROPE_DIM = 32
V_DIM = 64
MLA_WIDTH = MLA_HEADS * V_DIM
Q_LORA = 384
KV_LORA = 256
ROPE_THETA = 10000.0
ATTN_SCALE = 1.0 / math.sqrt(NOPE_DIM + ROPE_DIM)
CONV_CH = MIX_WIDTH - MLA_WIDTH
CONV_W = 31
CONV_STATE = CONV_W - 1
D_FF = -(-8 * D_MODEL // (3 * 256)) * 256
IN_WIDTH = Q_LORA + KV_LORA + ROPE_DIM + 2 * CONV_CH
EPS = 1e-6

kernel_name = "hybrid_mla_conformer_conv_stream_step"


def rms_norm(x, g):
    xf = x.astype(jnp.float32)
    y = xf * lax.rsqrt(jnp.mean(xf * xf, axis=-1, keepdims=True) + EPS)
    return y.astype(x.dtype) * g


def layer_norm(x, g, b):
    xf = x.astype(jnp.float32)
    mu = jnp.mean(xf, axis=-1, keepdims=True)
    xc = xf - mu
    y = xc * lax.rsqrt(jnp.mean(xc * xc, axis=-1, keepdims=True) + EPS)
    return y.astype(x.dtype) * g + b


def rope_angles(pos):
    inv = 1.0 / (ROPE_THETA ** (jnp.arange(0, ROPE_DIM, 2, dtype=jnp.float32) / ROPE_DIM))
    ang = pos.astype(jnp.float32)[:, None] * inv[None, :]
    return jnp.cos(ang), jnp.sin(ang)


def apply_rope(x, cos, sin):
    xf = x.astype(jnp.float32)
    x1, x2 = xf[..., : ROPE_DIM // 2], xf[..., ROPE_DIM // 2:]
    return jnp.concatenate([x1 * cos - x2 * sin, x2 * cos + x1 * sin], axis=-1).astype(x.dtype)


def latent_attend(q_lat, q_pe, c_kv, k_pe, mask):
    s = (jnp.einsum('bqhc,bkc->bhqk', q_lat, c_kv)
         + jnp.einsum('bqhr,bkr->bhqk', q_pe, k_pe)).astype(jnp.float32) * ATTN_SCALE
    if mask is not None:
        s = jnp.where(mask, s, -1e30)
    p = jax.nn.softmax(s, axis=-1).astype(c_kv.dtype)
    return jnp.einsum('bhqk,bkc->bqhc', p, c_kv)


def token_mixers(hn, pos, ckv_past, kpe_past, conv_past,
                 w_in, g_q, w_uq, g_kv, w_uk, w_uv, w_dw, b_dw, g_cn, b_cn, g_om, g_oc, w_out):
    B, S, _ = hn.shape
    proj = hn @ w_in
    cq, ckv, kpe, conv_in = jnp.split(
        proj, [Q_LORA, Q_LORA + KV_LORA, Q_LORA + KV_LORA + ROPE_DIM], axis=-1)
    cos, sin = rope_angles(pos)
    q = (rms_norm(cq, g_q) @ w_uq).reshape(B, S, MLA_HEADS, NOPE_DIM + ROPE_DIM)
    q_nope, q_pe = q[..., :NOPE_DIM], q[..., NOPE_DIM:]
    q_pe = apply_rope(q_pe, cos[:, None, :], sin[:, None, :])
    q_lat = jnp.einsum('bshd,chd->bshc', q_nope, w_uk)
    ckv = rms_norm(ckv, g_kv)
    kpe = apply_rope(kpe, cos, sin)
    a, gate = jnp.split(conv_in, 2, axis=-1)
    u = a * jax.nn.sigmoid(gate)
    if ckv_past is None:
        nb = S // Q_BLOCK
        k_chunk = jnp.arange(S) // CHUNK

        def block(args):
            qi, ql, qp = args
            q_chunk = (qi * Q_BLOCK + jnp.arange(Q_BLOCK)) // CHUNK
            mask = k_chunk[None, :] <= q_chunk[:, None]
            return latent_attend(ql, qp, ckv, kpe, mask)

        qlb = q_lat.reshape(B, nb, Q_BLOCK, MLA_HEADS, KV_LORA).swapaxes(0, 1)
        qpb = q_pe.reshape(B, nb, Q_BLOCK, MLA_HEADS, ROPE_DIM).swapaxes(0, 1)
        o_lat = lax.map(block, (jnp.arange(nb), qlb, qpb))
        o_lat = o_lat.swapaxes(0, 1).reshape(B, S, MLA_HEADS, KV_LORA)
        conv_full = jnp.pad(u, ((0, 0), (CONV_STATE, 0), (0, 0)))
    else:
        keys_c = jnp.concatenate([ckv_past, ckv], axis=1)
        keys_r = jnp.concatenate([kpe_past, kpe], axis=1)
        o_lat = latent_attend(q_lat, q_pe, keys_c, keys_r, None)
        conv_full = jnp.concatenate([conv_past, u], axis=1)
    o_mla = jnp.einsum('bshc,chd->bshd', o_lat, w_uv).reshape(B, S, MLA_WIDTH)
    dw = lax.conv_general_dilated(conv_full, w_dw[:, None, :], (1,), 'VALID',
                                  dimension_numbers=('NWC', 'WIO', 'NWC'),
                                  feature_group_count=CONV_CH) + b_dw
    conv_out = jax.nn.silu(layer_norm(dw, g_cn, b_cn))
    mixed = jnp.concatenate([rms_norm(o_mla, g_om), rms_norm(conv_out, g_oc)], axis=-1) @ w_out
    new_conv = conv_full[:, -CONV_STATE:]
    return mixed, ckv, kpe, new_conv


def encoder_layer(x, pos, ckv_past, kpe_past, conv_past, ln_mix, ln_ffn, w_gate, w_up, w_down, mix_params):
    mixed, ckv, kpe, new_conv = token_mixers(rms_norm(x, ln_mix), pos, ckv_past, kpe_past, conv_past, *mix_params)
    h = x + mixed
    f = rms_norm(h, ln_ffn)
    h = h + (jax.nn.silu(f @ w_gate) * (f @ w_up)) @ w_down
    return h, ckv, kpe, new_conv


def setup_inputs(seed: int = 0) -> dict:
    key = jax.random.key(seed)
    ks = jax.random.split(key, 32)
    n = lambda k, shape, s: jax.random.normal(k, shape, jnp.float32) * s
    gain = lambda k, shape: 1.0 + 0.05 * jax.random.normal(k, shape, jnp.float32)
    return {
        "x_prompt": n(ks[0], (BATCH, SEQ, D_MODEL), 1.0),
        "x_sample": n(ks[1], (DEC_BATCH, DEC_SEQ, D_MODEL), 1.0),
        "cache_kv_latent": n(ks[2], (DEPTH, DEC_BATCH, PAST_LEN, KV_LORA), 1.0),
        "cache_k_rope": n(ks[3], (DEPTH, DEC_BATCH, PAST_LEN, ROPE_DIM), 1.0),
        "state_conv": n(ks[4], (DEPTH, DEC_BATCH, CONV_STATE, CONV_CH), 0.5),
        "ln_mix": gain(ks[5], (DEPTH, D_MODEL)),
        "w_in": n(ks[6], (DEPTH, D_MODEL, IN_WIDTH), D_MODEL ** -0.5),
        "g_q": gain(ks[7], (DEPTH, Q_LORA)),
        "w_uq": n(ks[8], (DEPTH, Q_LORA, MLA_HEADS * (NOPE_DIM + ROPE_DIM)), Q_LORA ** -0.5),
        "g_kv": gain(ks[9], (DEPTH, KV_LORA)),
        "w_uk": n(ks[10], (DEPTH, KV_LORA, MLA_HEADS, NOPE_DIM), KV_LORA ** -0.5),
        "w_uv": n(ks[11], (DEPTH, KV_LORA, MLA_HEADS, V_DIM), KV_LORA ** -0.5),
        "w_dw": n(ks[12], (DEPTH, CONV_W, CONV_CH), CONV_W ** -0.5),
        "b_dw": n(ks[13], (DEPTH, CONV_CH), 0.02),
        "g_cn": gain(ks[14], (DEPTH, CONV_CH)),
        "b_cn": n(ks[15], (DEPTH, CONV_CH), 0.02),
        "g_om": gain(ks[16], (DEPTH, MLA_WIDTH)),
        "g_oc": gain(ks[17], (DEPTH, CONV_CH)),
        "w_out": n(ks[18], (DEPTH, MIX_WIDTH, D_MODEL), MIX_WIDTH ** -0.5),
        "ln_ffn": gain(ks[19], (DEPTH, D_MODEL)),
        "w_gate": n(ks[20], (DEPTH, D_MODEL, D_FF), D_MODEL ** -0.5),
        "w_up": n(ks[21], (DEPTH, D_MODEL, D_FF), D_MODEL ** -0.5),
        "w_down": n(ks[22], (DEPTH, D_FF, D_MODEL), D_FF ** -0.5),
        "g_final": gain(ks[23], (D_MODEL,)),
    }


def reference(x_prompt, x_sample, cache_kv_latent, cache_k_rope, state_conv,
              ln_mix, w_in, g_q, w_uq, g_kv, w_uk, w_uv, w_dw, b_dw, g_cn, b_cn, g_om, g_oc,
              w_out, ln_ffn, w_gate, w_up, w_down, g_final):
    past_len = cache_kv_latent.shape[2]
    pos_p = jnp.arange(x_prompt.shape[1])
    pos_s = past_len + jnp.arange(x_sample.shape[1])
    hp, hs = x_prompt, x_sample
    kv_p, kr_p, cv_p, kv_s, kr_s, cv_s = [], [], [], [], [], []
    for l in range(DEPTH):
        mix_params = (w_in[l], g_q[l], w_uq[l], g_kv[l], w_uk[l], w_uv[l], w_dw[l], b_dw[l],
                      g_cn[l], b_cn[l], g_om[l], g_oc[l], w_out[l])
        hp, a, b, c = encoder_layer(hp, pos_p, None, None, None, ln_mix[l], ln_ffn[l],
                                    w_gate[l], w_up[l], w_down[l], mix_params)
        kv_p.append(a); kr_p.append(b); cv_p.append(c)
        hs, a, b, c = encoder_layer(hs, pos_s, cache_kv_latent[l], cache_k_rope[l], state_conv[l],
                                    ln_mix[l], ln_ffn[l], w_gate[l], w_up[l], w_down[l], mix_params)
        kv_s.append(a); kr_s.append(b); cv_s.append(c)
    y_prompt = rms_norm(hp, g_final)
    y_sample = rms_norm(hs, g_final)
    return (y_prompt, y_sample,
            jnp.stack(kv_p), jnp.stack(kr_p), jnp.stack(cv_p),
            jnp.stack(kv_s), jnp.stack(kr_s), jnp.stack(cv_s))
```

```python
import math
from contextlib import ExitStack

import numpy as np
import ml_dtypes

import concourse.bass as bass
import concourse.mybir as mybir
from concourse.bass_utils import run_bass_kernel_spmd

F32, BF16, I32 = mybir.dt.float32, mybir.dt.bfloat16, mybir.dt.int32
AF, ALU = mybir.ActivationFunctionType, mybir.AluOpType

EPS = 1e-6
D = 1024
SEQ = 2048
NTILE = 16
DFF = 2816
NFF = 22
PAST = 4096
SCALE = 1.0 / math.sqrt(96.0)
N_CORES = 8
NDMA = 40
NDMA_SP = 28

C_LNMIX, C_GQ, C_GMIX, C_LNFFN, C_BDW, C_WDW, C_END = 0, 8, 11, 19, 27, 31, 155
B_GKV, B_GCN, B_BCN, B_GFIN, B_INV, B_END = 0, 256, 768, 1280, 2304, 2320


class Op:
    __slots__ = ("key", "val", "clk")

    def __init__(self, key, val, clk):
        self.key, self.val, self.clk = key, val, clk


class Sched:
    def __init__(self, nc, es):
        self.nc = nc
        self.E = {"pe": nc.tensor, "act": nc.scalar, "dve": nc.vector, "pool": nc.gpsimd, "sp": nc.sync}
        self.sem = {k: es.enter_context(nc.semaphore("sem_" + k)) for k in self.E}
        self.cnt = {k: 0 for k in self.E}
        self.clk = {k: {} for k in self.E}
        self.lastw, self.readers = {}, {}
        self.dsem = [es.enter_context(nc.semaphore(f"dsem{i}")) for i in range(NDMA)]
        self.dcnt = [0] * NDMA
        self.dlast = [None] * NDMA
        self.dnext = 0
        self.dnext_sw = 0
        self.out_ops = []
        self.hook = None
        self.hook_safe = True
        self.hook_rate = 1.0
        self.hook2 = None
        self._in_hook2 = False
        self.hook_acc = 0.0
        self.hook_steps = 0
        self.wkeys = {}
        self.hook_on = False
        self._in_hook = False

    def _semh(self, key):
        return self.sem[key] if isinstance(key, str) else self.dsem[key[1]]

    def _wait(self, eng, op):
        if op is None:
            return
        c = self.clk[eng]
        if c.get(op.key, 0) >= op.val:
            return
        if op.key == "pe" and eng == "pe":
            return
        self.E[eng].wait_ge(self._semh(op.key), op.val)
        for k, v in op.clk.items():
            if c.get(k, 0) < v:
                c[k] = v
        c[op.key] = op.val

    def _deps(self, reads, writes):
        deps = []
        for r in reads:
            w = self.lastw.get(r)
            if w is not None:
                deps.append(w)
        for k in writes:
            w = self.lastw.get(k)
            if w is not None:
                deps.append(w)
            deps.extend(self.readers.get(k, ()))
        return deps

    def _commit(self, op, reads, writes):
        for r in reads:
            self.readers.setdefault(r, []).append(op)
        for k in writes:
            self.lastw[k] = op
            self.readers[k] = []

    def op(self, eng, reads, writes, fn):
        for d in self._deps(reads, writes):
            self._wait(eng, d)
        inst = fn(self.E[eng])
        self.cnt[eng] += 1
        inst.then_inc(self.sem[eng], 1)
        o = Op(eng, self.cnt[eng], dict(self.clk[eng]))
        self._commit(o, reads, writes)
        if eng == "pe" and self.hook is not None and self.hook_on and not self._in_hook and not self._in_hook2:
            self.hook_acc += self.hook_rate
            while self.hook_acc >= 1.0:
                self.hook_acc -= 1.0
                self.step_hook()
        if eng == "pe" and self.hook2 is not None and not self._in_hook2 and not self._in_hook:
            self._in_hook2 = True
            try:
                next(self.hook2, None)
            finally:
                self._in_hook2 = False
        return o

    def step_hook(self):
        self._in_hook = True
        try:
            v = next(self.hook, None)
            self.hook_steps += 1
            self.hook_safe = True if v is None else bool(v)
        finally:
            self._in_hook = False

    def run_side(self, gen):
        self._in_hook2 = True
        try:
            for _ in gen:
                pass
        finally:
            self._in_hook2 = False

    def drain_to_safe(self):
        while self.hook is not None and not self.hook_safe:
            self.step_hook()

    def dma(self, q, out, in_, reads, writes, is_out=False, slow=False):
        if q == "pool":
            i = NDMA_SP + self.dnext_sw
            self.dnext_sw = (self.dnext_sw + 1) % (NDMA - NDMA_SP)
        else:
            i = self.dnext
            self.dnext = (i + 1) % NDMA_SP
        self._wait(q, self.dlast[i])
        for d in self._deps(reads, writes):
            self._wait(q, d)
        self.dcnt[i] += 16
        if slow:
            self.E[q].dma_start(out=out, in_=in_, allow_slow_non_contiguous=True).then_inc(self.dsem[i], 16)
        else:
            self.E[q].dma_start(out=out, in_=in_).then_inc(self.dsem[i], 16)
        o = Op(("d", i), self.dcnt[i], dict(self.clk[q]))
        self.dlast[i] = o
        self._commit(o, reads, writes)
        if is_out:
            self.out_ops.append(o)
        return o

    def finish(self):
        for o in self.out_ops:
            self._wait("sp", o)
        for e in ("pe", "act", "dve", "pool"):
            if self.cnt[e] > 0:
                self._wait("sp", Op(e, self.cnt[e], {}))


def bc_ins(a, pos, n):
    apl = [list(x) for x in a.ap]
    apl.insert(pos, [0, n])
    return bass.AP(a.tensor, a.offset, apl)


def build_program(do_sample=True, dbg=False):
    nc = bass.Bass("TRN2", target_bir_lowering=False)

    def din(name, shape, dt=F32):
        return nc.dram_tensor(name, list(shape), dt, kind="ExternalInput").ap()

    def dout(name, shape):
        return nc.dram_tensor(name, list(shape), F32, kind="ExternalOutput").ap()

    xp = din("xp", [SEQ, D])
    xs = din("xs", [32, D])
    ckv_c = din("ckv_c", [2, PAST, 256])
    ckr_c = din("ckr_c", [2, PAST, 32])
    cst = din("cst", [2, 30, 512])
    w_in = din("w_in", [D, 672])
    wc_r = din("wc_r", [8, 128, 8, 128])
    w_uq = din("w_uq", [384, 768])
    w_uk = din("w_uk", [256, 512])
    w_uv = din("w_uv", [256, 512])
    w_out = din("w_out", [D, D])
    wg_r = din("wg_r", [NFF, 128, 8, 128])
    wu_r = din("wu_r", [NFF, 128, 8, 128])
    w_down = din("w_down", [DFF, D])
    smallp = din("smallp", [128, C_END])
    bcp = din("bcp", [1, B_END])
    postab = din("postab", [128, 17])
    identd = din("identd", [128, 128], BF16)

    def dscr(name, shape):
        return nc.dram_tensor(name, list(shape), BF16).ap()

    wgb = dscr("wgb", [NFF * 128, 1024])
    wub = dscr("wub", [NFF * 128, 1024])
    wcb = dscr("wcb", [8 * 128, 1024])
    wob = dscr("wob", [D, D])
    wdb = dscr("wdb", [DFF, D])

    y_p = dout("y_p", [SEQ, D])
    y_s = dout("y_s", [32, D])
    kv_p = dout("kv_p", [SEQ, 256])
    kr_p = dout("kr_p", [SEQ, 32])
    cv_p = dout("cv_p", [30, 512])
    kv_s = dout("kv_s", [32, 256])
    kr_s = dout("kr_s", [32, 32])
    cv_s = dout("cv_s", [2, 30, 512])

    with ExitStack() as es:
        S = Sched(nc, es)

        def sb(name, shape, dt=F32):
            return es.enter_context(nc.sbuf_tensor(name, list(shape), dt))

        psF = es.enter_context(nc.psum_tensor("psF", [128, 6, 512], F32))
        psB = es.enter_context(nc.psum_tensor("psB", [128, 2, 1024], BF16))

        ident = sb("ident", [128, 128], BF16)
        sp_t = sb("sp_t", [128, C_END])
        bc_t = sb("bc_t", [128, B_END])
        pos_t = sb("pos_t", [128, 17])
        cos_t = sb("cos_t", [128, 17, 16])
        sin_t = sb("sin_t", [128, 17, 16])
        mhalf = sb("mhalf", [128, 1])
        scr = sb("scr", [128, 1])
        stat = sb("stat", [128, 512])
        w_in_a = sb("w_in_a", [128, 8, 672], BF16)
        w_uq_t = sb("w_uq_t", [128, 3, 768], BF16)
        w_uk_t = sb("w_uk_t", [128, 2, 512], BF16)
        w_uv_t = sb("w_uv_t", [128, 2, 512], BF16)
        KT = sb("KT", [128, 8, SEQ], BF16)
        VA = sb("VA", [128, NTILE, 8, 65], BF16)
        diag = sb("diag", [128, 31, 128], BF16)
        uTp = sb("uTp", [128, 4, 542], BF16)
        NXH = 8
        xh = [sb(f"xh{i}", [128, D]) for i in range(NXH)]
        tokb = [sb(f"tokb{i}", [128, D], BF16) for i in range(2)]
        actT = sb("actT", [128, 8, 512], BF16)
        aT = sb("aT", [128, 11, 512], BF16)
        fT = sb("fT", [128, 8, 512], BF16)
        fbuf = sb("fbuf", [128, D], BF16)
        t4 = sb("t4", [128, 512])
        wg = [sb(f"wg{i}", [128, 8, 128], BF16) for i in range(4)]
        wd = [sb(f"wd{i}", [128, 512], BF16) for i in range(5)]
        cqn = sb("cqn", [128, 384], BF16)
        cqT = sb("cqT", [128, 3, 128], BF16)
        Qtok = sb("Qtok", [128, 8, 96], BF16)
        ropec = sb("ropec", [128, 8, 2, 16])
        ropes = sb("ropes", [128, 8, 2, 16])
        ckvf = [sb(f"ckvf{i}", [128, 256]) for i in range(2)]
        ckvb = sb("ckvb", [128, 256], BF16)
        ckvT = sb("ckvT", [128, 2, 128], BF16)
        krf = [sb(f"krf{i}", [128, 32]) for i in range(2)]
        Ktok = sb("Ktok", [128, 8, 96], BF16)
        PT = [sb(f"PT{i}", [128, 4, 128], BF16) for i in range(2)]
        otok = sb("otok", [128, 512])
        rec = sb("rec", [128, 8])
        dwT = sb("dwT", [128, 4, 512], BF16)
        t1 = sb("t1", [128, 512])
        t2 = sb("t2", [128, 512])
        junk = sb("junk", [128, D], BF16)
        utok = sb("utok", [32, 512])
        KTs = [KT[:, b // 2, (b % 2) * 1024:(b % 2 + 1) * 1024].rearrange("p (h t) -> p h t", h=8) for b in range(4)]
        VAs = [KT[:, 2, b * 520:(b + 1) * 520].rearrange("p (h d) -> p h d", h=8) for b in range(3)] + \
              [KT[:, 3, 0:520].rearrange("p (h d) -> p h d", h=8)]
        VAn = [KT[:, 3, (1 + b) * 520:(2 + b) * 520].rearrange("p (h d) -> p h d", h=8) for b in range(2)]
        cbuf = [KT[:, 4, b * 256:(b + 1) * 256] for b in range(4)]
        ckvT2 = [KT[:, 4, 1024 + b * 256:1024 + (b + 1) * 256].rearrange("p (k t) -> p k t", k=2) for b in range(2)]
        krc = [KT[:, 5, b * 64:(b + 1) * 64].bitcast(F32) for b in range(4)]
        KTn = [KT[:, 6, b * 128:(b + 1) * 128].rearrange("p (h t) -> p h t", h=8) for b in range(2)]
        Ktok2 = [KT[:, 6, 256 + b * 768:256 + (b + 1) * 768].rearrange("p (h d) -> p h d", h=8) for b in range(2)]
        uTs = KT[:, 5, 1024:1024 + 368].rearrange("p (c s t) -> p c s t", c=4, s=2)

        wgS = [KT[:, r, c * 1024:(c + 1) * 1024].rearrange("p (k j) -> p k j", k=8) for r in range(6) for c in range(2)]
        wdS = [KT[:, 6 + b // 4, (b % 4) * 512:(b % 4 + 1) * 512] for b in range(8)]
        pools = {"wg": ([w_[:] for w_ in wg], [0], "wg"), "wgS": (wgS, [0], "wgS"), "wd": ([w_[:] for w_ in wd], [0], "wd"), "wdS": (wdS, [0], "wdS")}

        stat_i = [0]

        def newstat():
            i = stat_i[0] % 512
            stat_i[0] += 1
            return stat[:, i:i + 1], ("st", i)

        def rstd(ssq, ssq_k, n, P=128):
            t, tk = newstat()
            r, rk = newstat()
            S.op("dve", [ssq_k], [tk], lambda e: e.tensor_scalar(t[0:P], ssq[0:P], 1.0 / n, EPS, ALU.mult, ALU.add))
            S.op("pool", [tk, "mhalf"], [rk], lambda e: e.tensor_tensor(r[0:P], t[0:P], mhalf[0:P], ALU.pow))
            return r, rk

        def sumsq(src, src_keys, P, n, scale=1.0, excl=()):
            s, sk = newstat()
            S.op("act", list(src_keys), [sk, "junk"] + list(excl),
                 lambda e: e.activation(out=junk[0:P, 0:n], in_=src, func=AF.Square, scale=scale, accum_out=s[0:P]))
            return s, sk

        def transposes(src_fn, nk, P, width, dst, dst_keys, src_keys, bank, gain=None):
            pk = ("pb", bank)

            def f(e):
                last = None
                for k in range(nk):
                    last = e.transpose(psB[0:width, bank, k * P:(k + 1) * P], src_fn(k), ident[0:P, 0:P])
                return last
            S.op("pe", list(src_keys) + ["ident"], [pk], f)
            src = psB[0:width, bank, 0:nk * P].rearrange("p (k t) -> p k t", k=nk)
            if gain is None:
                S.op("act", [], [pk] + list(dst_keys), lambda e: e.copy(out=dst, in_=src))
            else:
                g = bc_ins(gain, 2, P)
                S.op("dve", ["sp_t"], [pk] + list(dst_keys), lambda e: e.tensor_tensor(out=dst, in0=src, in1=g, op=ALU.mult))

        S.dma("sp", ident[:], identd, [], ["ident"])
        S.dma("sp", sp_t[:], smallp, [], ["sp_t"])
        S.dma("sp", pos_t[:], postab, [], ["pos_t"])
        S.dma("sp", bc_t[:], bass.AP(bcp.tensor, 0, [[0, 128], [1, B_END]]), [], ["bc_t"])
        S.dma("pool", w_in_a[:], w_in.rearrange("(k p) c -> p k c", p=128), [], ["w_in_a"])
        S.dma("pool", w_uq_t[:], w_uq.rearrange("(k p) c -> p k c", p=128), [], ["w_uq"])
        S.dma("pool", w_uk_t[:], w_uk.rearrange("(k p) c -> p k c", p=128), [], ["w_uk"])
        S.dma("pool", w_uv_t[:], w_uv.rearrange("(k p) c -> p k c", p=128), [], ["w_uv"])
        def convert(items):
            for dst, src, key, rows in items:
                step = 512 if rows == 1024 else 704
                for r0 in range(0, rows, step):
                    S.dma("pool", dst[r0:r0 + step, :], src[r0:r0 + step, :], [], [(key, r0)])
                S.wkeys[key] = [(key, r0) for r0 in range(0, rows, step)]

        conv_late = [(wob, w_out, "wsc_o", 1024),
                     (wgb, wg_r.rearrange("c p k j -> (c p) (k j)"), "wsc_g", NFF * 128),
                     (wub, wu_r.rearrange("c p k j -> (c p) (k j)"), "wsc_u", NFF * 128),
                     (wdb, w_down, "wsc_d", DFF)]
        late_done = [False]
        S.op("dve", [], ["mhalf"], lambda e: e.memset(mhalf[:], -0.5))
        S.op("pool", [], [("VA", i) for i in range(NTILE)], lambda e: e.memset(VA[:, :, :, 64:65], 1.0))
        convert([(wcb, wc_r.rearrange("c p k j -> (c p) (k j)"), "wsc_c", 1024)])
        S.op("pool", [], ["uTp"], lambda e: e.memset(uTp[:], 0.0))
        S.op("dve", ["sp_t"], ["sp_t"],
             lambda e: e.tensor_scalar(sp_t[:, C_GMIX + 4:C_GMIX + 8], sp_t[:, C_GMIX + 4:C_GMIX + 8], 0.5, None, ALU.mult))

        TWO_PI = 2.0 * math.pi
        C1 = 6.28125
        C2 = TWO_PI - C1
        ang = t1[:, 0:272].rearrange("p (a b) -> p a b", a=17)
        wk = t2[:, 0:272].rearrange("p (a b) -> p a b", a=17)
        wk2 = t4[:, 0:272].rearrange("p (a b) -> p a b", a=17)
        ki = xh[0][:, 0:272].bitcast(I32).rearrange("p (a b) -> p a b", a=17)
        inv_b = bc_ins(bc_t[:, B_INV:B_INV + 16], 1, 17)
        pos_b = bc_ins(pos_t[:, 0:17], 2, 16)
        S.op("dve", ["bc_t", "pos_t"], ["t1"], lambda e: e.tensor_tensor(out=ang, in0=pos_b, in1=inv_b, op=ALU.mult))
        for tab, shift in ((sin_t, 0.0), (cos_t, math.pi / 2)):
            S.op("dve", ["t1"], ["t2"], lambda e: e.tensor_scalar(wk, ang, shift, 1.0 / TWO_PI, ALU.add, ALU.mult))
            S.op("dve", ["t2"], [("xh", 0)], lambda e: e.tensor_copy(out=ki, in_=wk))
            S.op("dve", [("xh", 0)], ["t2"], lambda e: e.tensor_copy(out=wk, in_=ki))
            S.op("dve", ["t2", "t1"], ["t4"], lambda e: e.scalar_tensor_tensor(out=wk2, in0=wk, scalar=-C1, in1=ang, op0=ALU.mult, op1=ALU.add))
            S.op("dve", ["t4", "t2"], ["t4"], lambda e: e.scalar_tensor_tensor(out=wk2, in0=wk, scalar=-C2, in1=wk2, op0=ALU.mult, op1=ALU.add))
            if shift != 0.0:
                S.op("dve", ["t4"], ["t4"], lambda e: e.tensor_scalar(wk2, wk2, shift, None, ALU.add))
            S.op("dve", ["t4"], ["t2"], lambda e: e.tensor_scalar(wk, wk2, math.pi, -TWO_PI, ALU.is_gt, ALU.mult))
            S.op("dve", ["t2", "t4"], ["t4"], lambda e: e.tensor_tensor(out=wk2, in0=wk2, in1=wk, op=ALU.add))
            S.op("dve", ["t4"], ["t2"], lambda e: e.tensor_scalar(wk, wk2, -math.pi, TWO_PI, ALU.is_lt, ALU.mult))
            S.op("dve", ["t2", "t4"], ["t4"], lambda e: e.tensor_tensor(out=wk2, in0=wk2, in1=wk, op=ALU.add))
            S.op("act", ["t4"], [("tab", id(tab))], lambda e, tab=tab: e.activation(out=tab[:], in_=wk2, func=AF.Sin))
        TABK = [("tab", id(sin_t)), ("tab", id(cos_t))]

        def rope(src, P, ti, out_lo, out_hi, nh, keys_r, keys_w):
            cb = bc_ins(bc_ins(cos_t[0:P, ti, :], 1, 2), 1, nh)
            sbb = bc_ins(bc_ins(sin_t[0:P, ti, :], 1, 2), 1, nh)
            rc = ropec[0:P, 0:nh]
            rs = ropes[0:P, 0:nh]
            S.op("dve", TABK, list(keys_r) + ["ropec"], lambda e: e.tensor_tensor(out=rc, in0=src, in1=cb, op=ALU.mult))
            S.op("dve", TABK, list(keys_r) + ["ropes"], lambda e: e.tensor_tensor(out=rs, in0=src, in1=sbb, op=ALU.mult))
            S.op("dve", ["ropec", "ropes"], list(keys_w),
                 lambda e: e.tensor_tensor(out=out_lo, in0=ropec[0:P, 0:nh, 0, :], in1=ropes[0:P, 0:nh, 1, :], op=ALU.subtract))
            S.op("dve", ["ropec", "ropes"], list(keys_w),
                 lambda e: e.tensor_tensor(out=out_hi, in0=ropec[0:P, 0:nh, 1, :], in1=ropes[0:P, 0:nh, 0, :], op=ALU.add))

        xh_i = [0]

        def load_x(x_ap, P):
            s_ = xh_i[0] % NXH
            xh_i[0] += 1
            S.dma("pool", xh[s_][0:P, :], x_ap, [], [("xh", s_)])
            return s_
        tokb_i = [0]

        def front(x_src, P, col0, ti, kv_dst, kr_dst, kt_dst, va_dst, kt_keys, va_keys, qt, qtk):
            s = x_src
            xk = ("xh", s)
            xt = xh[s]
            ssq, ssqk = sumsq(xt[0:P, :], [xk], P, D)
            r, rk = rstd(ssq, ssqk, D, P)
            tb = tokb_i[0] % 2
            tokb_i[0] += 1
            hn = tokb[tb]
            S.op("act", [xk, rk], [("tokb", tb, 0), ("tokb", tb, 1)], lambda e: e.activation(out=hn[0:P, :], in_=xt[0:P, :], func=AF.Copy, scale=r[0:P]))
            transposes(lambda k: hn[0:P, k * 128:(k + 1) * 128], 8, P, 128, actT[:, :, col0:col0 + P], [("actT", col0)],
                       [("tokb", tb, 0), ("tokb", tb, 1)], 0, gain=sp_t[:, C_LNMIX:C_LNMIX + 8])
            def fa(e):
                last = None
                for k in range(8):
                    e.matmul(psF[0:P, 0, 0:384], lhsT=actT[:, k, col0:col0 + P], rhs=w_in_a[:, k, 0:384], start=(k == 0), stop=(k == 7))
                for k in range(8):
                    last = e.matmul(psF[0:P, 1, 0:288], lhsT=actT[:, k, col0:col0 + P], rhs=w_in_a[:, k, 384:672], start=(k == 0), stop=(k == 7))
                return last
            S.op("pe", [("actT", col0), "w_in_a"], [("pf", 0), ("pf", 1)], fa)
            sq, sqk = sumsq(psF[0:P, 0, 0:384], [], P, 384, excl=[("pf", 0)])
            rq, rqk = rstd(sq, sqk, 384, P)
            S.op("act", [rqk], [("pf", 0), "cqn"], lambda e: e.activation(out=cqn[0:P, :], in_=psF[0:P, 0, 0:384], func=AF.Copy, scale=rq[0:P]))
            transposes(lambda k: cqn[0:P, k * 128:(k + 1) * 128], 3, P, 128, cqT[:, :, 0:P], ["cqT"], ["cqn"], 0,
                       gain=sp_t[:, C_GQ:C_GQ + 3])
            sk_, skk = sumsq(psF[0:P, 1, 0:256], [], P, 256, excl=[("pf", 1)])
            rkv, rkvk = rstd(sk_, skk, 256, P)
            cb_i = ti % 2
            cf = ckvf[cb_i]
            S.op("dve", [rkvk, "bc_t"], [("pf", 1), ("ckvf", cb_i)],
                 lambda e: e.scalar_tensor_tensor(out=cf[0:P, :], in0=psF[0:P, 1, 0:256], scalar=rkv[0:P], in1=bc_t[0:P, B_GKV:B_GKV + 256], op0=ALU.mult, op1=ALU.mult))
            S.op("act", [("ckvf", cb_i)], ["ckvb"], lambda e: e.copy(out=ckvb[0:P, :], in_=cf[0:P, :]))
            kf = krf[cb_i]
            ksrc = psF[0:P, 1, 256:288].rearrange("p (a h d) -> p a h d", a=1, h=2)
            rope(ksrc, P, ti, kf[0:P, 0:16].rearrange("p (a d) -> p a d", a=1), kf[0:P, 16:32].rearrange("p (a d) -> p a d", a=1), 1,
                 [("pf", 1)], [("krf", cb_i)])
            def fq(e):
                last = None
                for hb in range(2):
                    for h4 in range(4):
                        hh = hb * 4 + h4
                        for k in range(3):
                            last = e.matmul(psF[0:P, hb, h4 * 128:h4 * 128 + 96], lhsT=cqT[:, k, 0:P], rhs=w_uq_t[:, k, hh * 96:(hh + 1) * 96],
                                            start=(k == 0), stop=(k == 2))
                return last
            S.op("pe", ["cqT", "w_uq"], [("pf", 0), ("pf", 1)], fq)
            qv = psF[0:P, 0:2, :].rearrange("p b (h d) -> p (b h) d", h=4)
            S.op("act", [], [("pf", 0), ("pf", 1), "Qtok"], lambda e: e.copy(out=Qtok[0:P, :, 0:64], in_=qv[:, :, 0:64]))
            qpe = qv[:, :, 64:96].rearrange("p h (a d) -> p h a d", a=2)
            rope(qpe, P, ti, Qtok[0:P, :, 64:80], Qtok[0:P, :, 80:96], 8, [("pf", 0), ("pf", 1)], ["Qtok"])
            transposes(lambda h: Qtok[0:P, h, :], 8, P, 96, qt[0:96, :, 0:P], [qtk], ["Qtok"], 0)
            transposes(lambda k: ckvb[0:P, k * 128:(k + 1) * 128], 2, P, 128, ckvT[:, :, 0:P], ["ckvT"], ["ckvb"], 0)
            kv_from_ckvT(P, kf, ("krf", cb_i), kt_dst, va_dst, kt_keys, va_keys, 0, 1)
            S.dma("pool", kv_dst, cf[0:P, :], [("ckvf", cb_i)], [], is_out=True)
            S.dma("pool", kr_dst, kf[0:P, :], [("krf", cb_i)], [], is_out=True)
            return s

        def kv_from_ckvT(P, kf, kfk, kt_dst, va_dst, kt_keys, va_keys, bk, bv):
            def fk(e):
                last = None
                for k in range(2):
                    e.matmul(psF[0:P, bk, :], lhsT=ckvT[:, k, 0:P], rhs=w_uk_t[:, k, :], start=(k == 0), stop=(k == 1))
                for k in range(2):
                    last = e.matmul(psF[0:P, bv, :], lhsT=ckvT[:, k, 0:P], rhs=w_uv_t[:, k, :], start=(k == 0), stop=(k == 1))
                return last
            S.op("pe", ["ckvT", "w_uk", "w_uv"], [("pf", bk), ("pf", bv)], fk)
            S.op("act", [], [("pf", bk), "Ktok"], lambda e: e.copy(out=Ktok[0:P, :, 0:64], in_=psF[0:P, bk, :].rearrange("p (h d) -> p h d", h=8)))
            S.op("dve", [kfk], ["Ktok"], lambda e: e.tensor_copy(out=Ktok[0:P, :, 64:96], in_=bc_ins(kf[0:P, :], 1, 8)))
            S.op("dve", [], [("pf", bv)] + list(va_keys), lambda e: e.tensor_copy(out=va_dst, in_=psF[0:P, bv, :].rearrange("p (h d) -> p h d", h=8)))
            transposes(lambda h: Ktok[0:P, h, :], 8, P, 96, kt_dst, kt_keys, ["Ktok"], 0)

        pt_i = [0]

        def attn_prompt(i, tb, qt, qtk):
            nk = i + 1
            glist = [(h, list(range(g, min(g + 4, nk)))) for h in range(8) for g in range(0, nk, 4)]

            def emit_s(idx):
                h, g = glist[idx]
                bb = idx % 2

                def fs(e):
                    last = None
                    for j, kt in enumerate(g):
                        last = e.matmul(psF[:, bb, j * 128:(j + 1) * 128], lhsT=KT[0:96, h, kt * 128:(kt + 1) * 128], rhs=qt[0:96, h, :], start=True, stop=True)
                    return last
                S.op("pe", [qtk] + [("KT", kt) for kt in g], [("pf", bb)], fs)
                n = len(g) * 128
                S.op("act", [], [("pf", bb), ("PT", bb)],
                     lambda e: e.activation(out=PT[bb][:].rearrange("p a b -> p (a b)")[:, 0:n], in_=psF[:, bb, 0:n], func=AF.Exp, scale=SCALE))
                if i in g:
                    j = g.index(i)
                    S.op("pool", [], [("PT", bb)], lambda e: e.memset(PT[bb][64:128, j, 0:64], 0.0))

            def emit_pv(idx):
                h, g = glist[idx]
                bb = idx % 2
                pob = 2 + h // 4
                pocol = (h % 4) * 128

                def fp(e):
                    last = None
                    for j, kt in enumerate(g):
                        last = e.matmul(psF[:, pob, pocol:pocol + 65], lhsT=PT[bb][:, j, :], rhs=VA[:, kt, h, :], start=(kt == 0), stop=(kt == nk - 1))
                    return last
                S.op("pe", [("PT", bb)] + [("VA", kt) for kt in g], [("pf", pob)], fp)

            for idx in range(len(glist)):
                emit_s(idx)
                if idx >= 1:
                    emit_pv(idx - 1)
            emit_pv(len(glist) - 1)
            pov = psF[:, 2:4, :].rearrange("p b (h d) -> p (b h) d", h=4)
            S.op("dve", [], [("pf", 2), ("pf", 3), "rec"], lambda e: e.reciprocal(out=rec[:].rearrange("p (h a) -> p h a", a=1), in_=pov[:, :, 64:65]))
            S.op("dve", ["rec"], [("pf", 2), ("pf", 3), "otok"],
                 lambda e: e.tensor_tensor(out=otok[:].rearrange("p (h d) -> p h d", h=8), in0=pov[:, :, 0:64], in1=bc_ins(rec[:], 2, 64), op=ALU.mult))
            so, sok = sumsq(otok[:], ["otok"], 128, 512)
            ro, rok = rstd(so, sok, 512)
            S.op("act", ["otok", rok], [("tokb", tb, 0)], lambda e: e.activation(out=tokb[tb][:, 0:512], in_=otok[:], func=AF.Copy, scale=ro[:]))


        def attn_sample(sq, tb, qt, qtk):
            NKT = PAST // 128
            first = [True, True]

            def st_load(kt):
                b = kt % 4
                S.dma("pool", cbuf[b], ckv_c[sq, kt * 128:(kt + 1) * 128, :], [], [("cbuf", b)])
                S.dma("sp", krc[b], ckr_c[sq, kt * 128:(kt + 1) * 128, :], [], [("krc", b)])

            def st_t1(kt):
                b, c2 = kt % 4, kt % 2
                transposes(lambda k: cbuf[b][:, k * 128:(k + 1) * 128], 2, 128, 128, ckvT2[c2], [("ckvT2", c2)], [("cbuf", b)], 0)

            def st_m(kt):
                b, c2 = kt % 4, kt % 2

                def fk(e):
                    last = None
                    for k in range(2):
                        e.matmul(psF[:, 0, :], lhsT=ckvT2[c2][:, k, :], rhs=w_uk_t[:, k, :], start=(k == 0), stop=(k == 1))
                    for k in range(2):
                        last = e.matmul(psF[:, 1, :], lhsT=ckvT2[c2][:, k, :], rhs=w_uv_t[:, k, :], start=(k == 0), stop=(k == 1))
                    return last
                S.op("pe", [("ckvT2", c2), "w_uk", "w_uv"], [("pf", 0), ("pf", 1)], fk)
                S.op("act", [], [("pf", 0), ("Ktok2", c2)], lambda e: e.copy(out=Ktok2[c2][:, :, 0:64], in_=psF[:, 0, :].rearrange("p (h d) -> p h d", h=8)))
                S.op("dve", [("krc", b)], [("Ktok2", c2)], lambda e: e.tensor_copy(out=Ktok2[c2][:, :, 64:96], in_=bc_ins(krc[b], 1, 8)))
                S.op("dve", [], [("pf", 1), ("VAs", b)], lambda e: e.tensor_copy(out=VAs[b][:, :, 0:64], in_=psF[:, 1, :].rearrange("p (h d) -> p h d", h=8)))

            def st_t2(kt):
                b, c2 = kt % 4, kt % 2
                transposes(lambda h: Ktok2[c2][:, h, :], 8, 128, 96, KTs[b][0:96, :, :], [("KTs", b)], [("Ktok2", c2)], 0)

            def bufs(kt):
                if kt == NKT:
                    return KTn[sq], ("KTn", sq), VAn[sq], ("VAn", sq), 16
                b = kt % 4
                return KTs[b], ("KTs", b), VAs[b], ("VAs", b), 128

            def st_b(kt):
                kt_t, kt_k, _, _, KP = bufs(kt)
                bb = kt % 2

                def fs(e):
                    last = None
                    for h in range(8):
                        last = e.matmul(psF[0:KP, bb, h * 16:(h + 1) * 16], lhsT=kt_t[0:96, h, 0:KP], rhs=qt[0:96, h, 0:16], start=True, stop=True)
                    return last
                S.op("pe", [qtk, kt_k], [("pf", bb)], fs)
                S.op("act", [], [("pf", bb), ("PT", bb)],
                     lambda e: e.activation(out=PT[bb][0:KP, 0, :], in_=psF[0:KP, bb, 0:128], func=AF.Exp, scale=SCALE))

            def st_c(kt):
                _, _, va_t, va_k, KP = bufs(kt)
                bb = kt % 2
                new = kt == NKT

                def fp(e):
                    last = None
                    for h in range(8):
                        st = first[h // 4]
                        first[h // 4] = False
                        last = e.matmul(psF[0:16, 2 + h // 4, (h % 4) * 128:(h % 4) * 128 + 65], lhsT=PT[bb][0:KP, 0, h * 16:(h + 1) * 16],
                                        rhs=va_t[0:KP, h, :], start=st, stop=new, skip_group_check=True)
                    return last
                S.op("pe", [("PT", bb), va_k], [("pf", 2), ("pf", 3)], fp)

            for step in range(NKT + 6):
                if 0 <= step - 5 <= NKT:
                    st_c(step - 5)
                if 0 <= step - 4 <= NKT:
                    st_b(step - 4)
                if 0 <= step - 3 < NKT:
                    st_t2(step - 3)
                if 0 <= step - 2 < NKT:
                    st_m(step - 2)
                if 0 <= step - 1 < NKT:
                    st_t1(step - 1)
                if step < NKT:
                    st_load(step)
            pov = psF[0:16, 2:4, :].rearrange("p b (h d) -> p (b h) d", h=4)
            S.op("dve", [], [("pf", 2), ("pf", 3), "rec"], lambda e: e.reciprocal(out=rec[0:16, :].rearrange("p (h a) -> p h a", a=1), in_=pov[:, :, 64:65]))
            S.op("dve", ["rec"], [("pf", 2), ("pf", 3), "otok"],
                 lambda e: e.tensor_tensor(out=otok[0:16, :].rearrange("p (h d) -> p h d", h=8), in0=pov[:, :, 0:64], in1=bc_ins(rec[0:16, :], 2, 64), op=ALU.mult))
            so, sok = sumsq(otok[0:16, :], ["otok"], 16, 512)
            ro, rok = rstd(so, sok, 512, 16)
            S.op("act", ["otok", rok], [("tokb", tb, 0)], lambda e: e.activation(out=tokb[tb][0:16, 0:512], in_=otok[0:16, :], func=AF.Copy, scale=ro[0:16]))

        wg_i = [0]
        wd_i = [0]
        NWG = 4

        def fm_chunk(wsrc_r, c, T, bank, rhsT, rkeys):
            bufs_, cnt_, kn = pools["wg"]
            wb = cnt_[0] % len(bufs_)
            cnt_[0] += 1
            S.dma("sp", wg[wb][:], wsrc_r[0][c * 128:(c + 1) * 128, :].rearrange("p (k j) -> p k j", k=8), S.wkeys[wsrc_r[1]], [("wg", wb)])

            def f(e):
                last = None
                for k in range(8):
                    last = e.matmul(psF[:, bank, 0:T], lhsT=wg[wb][:, k, :], rhs=rhsT[:, k, 0:T], start=(k == 0), stop=(k == 7))
                return last
            S.op("pe", [("wg", wb)] + rkeys, [("pf", bank)], f)
            return wb

        def fm_chunk_g(wsrc_r, c, T, bank, rhsT, rkeys, pool="wg"):
            bufs_, cnt_, kn = pools[pool]
            wb = cnt_[0] % len(bufs_)
            cnt_[0] += 1
            wt = bufs_[wb]
            S.dma("sp", wt, wsrc_r[0][c * 128:(c + 1) * 128, :].rearrange("p (k j) -> p k j", k=8), S.wkeys[wsrc_r[1]], [(kn, wb)])
            for k0 in (0, 4):
                def f(e, k0=k0):
                    last = None
                    for k in range(k0, k0 + 4):
                        last = e.matmul(psF[:, bank, 0:T], lhsT=wt[:, k, :], rhs=rhsT[:, k, 0:T], start=(k == 0), stop=(k == 7))
                    return last
                S.op("pe", [(kn, wb)] + rkeys, [("pf", bank)], f)
                yield False

        def tm_pass(wsrc, krows, half, lhs_fn, lhs_keys, tiles, banks, pool="wd"):
            nk = len(krows)
            bufs_, cnt_, kn = pools[pool]
            for kk, kr in enumerate(krows):
                b = cnt_[0] % len(bufs_)
                cnt_[0] += 1
                wt = bufs_[b]
                S.dma("sp", wt, wsrc[0][kr * 128:(kr + 1) * 128, half * 512:(half + 1) * 512], S.wkeys[wsrc[1]], [(kn, b)])

                def f(e, wt=wt, kk=kk):
                    last = None
                    for (P, c0), bank in zip(tiles, banks):
                        last = e.matmul(psF[0:P, bank, :], lhsT=lhs_fn(kk, P, c0), rhs=wt, start=(kk == 0), stop=(kk == nk - 1))
                    return last
                S.op("pe", [(kn, b)] + list(lhs_keys), [("pf", bank) for bank in banks[:len(tiles)]], f)
                yield False

        def resid_add(slot, P, bank, half):
            xt = xh[slot]
            S.op("dve", [], [("pf", bank), ("xh", slot)],
                 lambda e: e.tensor_tensor(out=xt[0:P, half * 512:(half + 1) * 512], in0=psF[0:P, bank, :], in1=xt[0:P, half * 512:(half + 1) * 512], op=ALU.add))

        def block_x(tiles, T, blk, is_prompt):
            akeys = [("actT", td["col0"]) for td in tiles]
            HM = 'fao'
            S.hook_on = 'f' in HM
            S.hook_rate = 3.0
            pre = [load_x(td["x_ap"], td["P"]) for td in tiles]
            slots = []
            for j, td in enumerate(tiles):
                slots.append(td["front"](pre[j]))
                if not late_done[0]:
                    late_done[0] = True
                    convert(conv_late)
            S.hook_rate = 2.0
            last30 = is_prompt and blk == 3
            use45 = last30 or not is_prompt
            if use45:
                S.hook_on = False
                S.drain_to_safe()
            if is_prompt and blk > 0:
                S.op("act", ["uTp"], ["uTp"], lambda e: e.copy(out=uTp[:, :, 0:30], in_=uTp[:, :, 512:542]))
            if not is_prompt:
                for sq in range(2):
                    S.dma("sp", utok[0:30, :], cst[sq], [], ["utok"])
                    S.dma("pool", cv_s[sq, 0:14, :], cst[sq, 16:30, :], [], [], is_out=True)
                    S.op("act", ["utok"], [("dwT", 0)], lambda e: e.copy(out=dwT[0:30, 0, :], in_=utok[0:30, :]))
                    transposes(lambda k: dwT[0:30, 0, k * 128:(k + 1) * 128], 4, 30, 128, uTs[:, :, sq, 0:30], ["uTs"], [("dwT", 0)], 1)

            def tokmajor_rows(c, wb, rows0, nrows):
                def ft(e):
                    last = None
                    for k in range(8):
                        last = e.matmul(psF[0:nrows, 2 + c // 4, (c % 4) * 128:(c % 4 + 1) * 128], lhsT=actT[:, k, rows0:rows0 + nrows],
                                        rhs=wg[wb][:, k, :], start=(k == 0), stop=(k == 7))
                    return last
                S.op("pe", [("wg", wb)] + akeys, [("pf", 2 + c // 4)], ft)

            for ch in range(4):
                ab, gb = ch % 2, (4 if use45 else 2) + ch % 2
                wba = fm_chunk((wcb, "wsc_c"), ch, T, ab, actT, akeys)
                if last30:
                    tokmajor_rows(ch, wba, 482, 30)
                if not is_prompt:
                    tokmajor_rows(ch, wba, 0, 32)
                wbg = fm_chunk((wcb, "wsc_c"), 4 + ch, T, gb, actT, akeys)
                if last30:
                    tokmajor_rows(4 + ch, wbg, 482, 30)
                if not is_prompt:
                    tokmajor_rows(4 + ch, wbg, 0, 32)
                gt, gk = (t1, "t1") if ch % 2 == 0 else (otok, "otok")
                S.op("act", [], [("pf", gb), gk], lambda e, gb=gb, gt=gt: e.activation(out=gt[:, 0:T], in_=psF[:, gb, 0:T], func=AF.Tanh, scale=0.5))
                S.op("dve", [gk], [("pf", ab), gk],
                     lambda e, ab=ab, gt=gt: e.scalar_tensor_tensor(out=gt[:, 0:T], in0=gt[:, 0:T], scalar=1.0, in1=psF[:, ab, 0:T], op0=ALU.add, op1=ALU.mult))
                if is_prompt:
                    S.op("act", [gk], ["uTp"], lambda e, ch=ch, gt=gt: e.activation(out=uTp[:, ch, 30:30 + T], in_=gt[:, 0:T], func=AF.Copy, scale=0.5))
                else:
                    S.op("act", [gk], ["uTs"], lambda e, ch=ch, gt=gt: e.activation(out=uTs[:, ch, :, 30:46], in_=gt[:, 0:32].rearrange("p (s t) -> p s t", s=2), func=AF.Copy, scale=0.5))
            nr = 30 if last30 else (32 if not is_prompt else 0)
            if nr:
                S.op("act", [], [("pf", 3), "t1"], lambda e: e.activation(out=t1[0:nr, :], in_=psF[0:nr, 3, :], func=AF.Tanh, scale=0.5))
                S.op("dve", ["t1"], [("pf", 2), "t2"],
                     lambda e: e.scalar_tensor_tensor(out=t2[0:nr, :], in0=t1[0:nr, :], scalar=1.0, in1=psF[0:nr, 2, :], op0=ALU.add, op1=ALU.mult))
                S.op("dve", ["t2"], ["utok"], lambda e: e.tensor_scalar(utok[0:nr, :], t2[0:nr, :], 0.5, None, ALU.mult))
                if last30:
                    S.dma("pool", cv_p, utok[0:30, :], ["utok"], [], is_out=True)
                else:
                    for sq in range(2):
                        S.dma("pool", cv_s[sq, 14:30, :], utok[sq * 16:(sq + 1) * 16, :], ["utok"], [], is_out=True)
            HALF = ((0, 16), (16, 31))
            for ch in range(4):
                wv = sp_t[:, C_WDW + ch * 31:C_WDW + (ch + 1) * 31]
                for hi, (j0, j1) in enumerate(HALF):
                    S.op("dve", ["sp_t", "ident"], [("diag", hi)],
                         lambda e, wv=wv, j0=j0, j1=j1: e.tensor_tensor(out=diag[:, j0:j1, :], in0=bc_ins(ident[:], 1, j1 - j0),
                                                                        in1=bc_ins(wv[:, j0:j1], 2, 128), op=ALU.mult))
                bank = (4 if use45 else 0) + ch % 2
                for hi, (j0, j1) in enumerate(HALF):
                    def fc(e, ch=ch, bank=bank, j0=j0, j1=j1):
                        last = None
                        if is_prompt:
                            for j in range(j0, j1):
                                last = e.matmul(psF[:, bank, 0:T], lhsT=diag[:, j, :], rhs=uTp[:, ch, j:j + T], start=(j == 0), stop=(j == 30))
                        else:
                            for sq in range(2):
                                for j in range(j0, j1):
                                    last = e.matmul(psF[:, bank, sq * 16:(sq + 1) * 16], lhsT=diag[:, j, :], rhs=uTs[:, ch, sq, j:j + 16],
                                                    start=(j == 0 and sq == 0), stop=(j == 30), skip_group_check=True)
                        return last
                    S.op("pe", [("diag", hi), "uTp", "uTs"], [("pf", bank)], fc)
                S.op("act", ["sp_t"], [("pf", bank), ("dwT", ch)],
                     lambda e, ch=ch, bank=bank: e.activation(out=dwT[:, ch, 0:T], in_=psF[:, bank, 0:T], func=AF.Identity, bias=sp_t[:, C_BDW + ch:C_BDW + ch + 1]))
            S.hook_on = 'a' in HM
            est_ops = sum(td["nops"] for td in tiles) + 16
            S.hook_rate = min(3.0, max(0.2, (S.hook_total - S.hook_steps) / float(est_ops)))
            def gen_convln(td, tb):
                P, c0 = td["P"], td["col0"]
                pk = ("pb", 1)

                def ftr(e):
                    last = None
                    for ch in range(4):
                        last = e.transpose(psB[0:P, 1, ch * 128:(ch + 1) * 128], dwT[:, ch, c0:c0 + P], ident[:, :])
                    return last
                S.op("pe", [("dwT", ch) for ch in range(4)] + ["ident"], [pk], ftr)
                s1, s1k = newstat()
                S.op("act", [], [pk, "t1", s1k], lambda e: e.activation(out=t1[0:P, :], in_=psB[0:P, 1, 0:512], func=AF.Copy, accum_out=s1[0:P]))
                yield
                s2, s2k = sumsq(t1[0:P, :], ["t1"], P, 512)
                mean, mk = newstat()
                msq, msk = newstat()
                ve, vek = newstat()
                rs_, rsk = newstat()
                S.op("dve", [s1k], [mk], lambda e: e.tensor_scalar(mean[0:P], s1[0:P], 1.0 / 512, None, ALU.mult))
                S.op("dve", [mk], [msk], lambda e: e.tensor_tensor(out=msq[0:P], in0=mean[0:P], in1=mean[0:P], op=ALU.mult))
                yield
                S.op("dve", [s2k, msk], [vek], lambda e: e.scalar_tensor_tensor(out=ve[0:P], in0=s2[0:P], scalar=1.0 / 512, in1=msq[0:P], op0=ALU.mult, op1=ALU.subtract))
                S.op("dve", [vek], [vek], lambda e: e.tensor_scalar(ve[0:P], ve[0:P], EPS, None, ALU.add))
                S.op("pool", [vek, "mhalf"], [rsk], lambda e: e.tensor_tensor(rs_[0:P], ve[0:P], mhalf[0:P], ALU.pow))
                yield
                S.op("dve", [mk, rsk, "t1"], ["t1"], lambda e: e.tensor_scalar(t1[0:P, :], t1[0:P, :], mean[0:P], rs_[0:P], ALU.subtract, ALU.mult))
                yield
                S.op("dve", ["t1", "bc_t"], ["t1"], lambda e: e.tensor_tensor(out=t1[0:P, :], in0=t1[0:P, :], in1=bc_t[0:P, B_GCN:B_GCN + 512], op=ALU.mult))
                yield
                S.op("dve", ["t1", "bc_t"], ["t1"], lambda e: e.tensor_tensor(out=t1[0:P, :], in0=t1[0:P, :], in1=bc_t[0:P, B_BCN:B_BCN + 512], op=ALU.add))
                yield
                S.op("act", ["t1"], ["t2"], lambda e: e.activation(out=t2[0:P, :], in_=t1[0:P, :], func=AF.Tanh, scale=0.5))
                yield
                S.op("dve", ["t1", "t2"], ["t2"], lambda e: e.scalar_tensor_tensor(out=t2[0:P, :], in0=t2[0:P, :], scalar=1.0, in1=t1[0:P, :], op0=ALU.add, op1=ALU.mult))
                yield
                sc, sck = sumsq(t2[0:P, :], ["t2"], P, 512, scale=0.5)
                rc, rck = rstd(sc, sck, 512, P)
                yield
                S.op("act", ["t2", rck], [("tokb", tb, 1)], lambda e: e.activation(out=tokb[tb][0:P, 512:1024], in_=t2[0:P, :], func=AF.Copy, scale=rc[0:P]))
                yield

            tbs = []
            for td in tiles:
                tbs.append(tokb_i[0] % 2)
                tokb_i[0] += 1
            S.run_side(gen_convln(tiles[0], tbs[0]))
            for j, td in enumerate(tiles):
                P, c0 = td["P"], td["col0"]
                tb = tbs[j]
                S.hook2 = gen_convln(tiles[j + 1], tbs[j + 1]) if j + 1 < len(tiles) else None
                td["attn"](tb)
                if S.hook2 is not None:
                    S.run_side(S.hook2)
                    S.hook2 = None
                transposes(lambda k, tb=tb, P=P: tokb[tb][0:P, k * 128:(k + 1) * 128], 8, P, 128, actT[:, :, c0:c0 + P], [("actT", c0)],
                           [("tokb", tb, 0), ("tokb", tb, 1)], 0, gain=sp_t[:, C_GMIX:C_GMIX + 8])
            tl = [(td["P"], td["col0"]) for td in tiles]
            S.hook_on = 'o' in HM
            for half in range(2):
                for _ in tm_pass((wob, "wsc_o"), list(range(8)), half, lambda k, P, c0: actT[:, k, c0:c0 + P], akeys, tl, [0, 1, 2, 3]):
                    pass
                for t, (P, c0) in enumerate(tl):
                    resid_add(slots[t], P, t, half)
            S.hook_on = False
            return slots

        def gen_ffn(tiles, slots, T, deep=False):
            gp, dp = ("wgS", "wdS") if deep else ("wg", "wd")
            fkeys = [("fT", td["col0"]) for td in tiles]
            tl = [(td["P"], td["col0"]) for td in tiles]
            for t, td in enumerate(tiles):
                P, c0 = td["P"], td["col0"]
                xt = xh[slots[t]]
                sf, sfk = sumsq(xt[0:P, :], [("xh", slots[t])], P, D)
                rf, rfk = rstd(sf, sfk, D, P)
                S.op("act", [("xh", slots[t]), rfk], ["fbuf"],
                     lambda e, P=P, xt=xt, rf=rf: e.activation(out=fbuf[0:P, :], in_=xt[0:P, :], func=AF.Copy, scale=rf[0:P]))
                transposes(lambda k, P=P: fbuf[0:P, k * 128:(k + 1) * 128], 8, P, 128, fT[:, :, c0:c0 + P], [("fT", c0)],
                           ["fbuf"], 1, gain=sp_t[:, C_LNFFN:C_LNFFN + 8])
                yield True
            pairs = [list(range(i, min(i + 2, len(tiles)))) for i in range(0, len(tiles), 2)]
            for r in range(2):
                for cc in range(11):
                    c = r * 11 + cc
                    yield from fm_chunk_g((wgb, "wsc_g"), c, T, 4, fT, fkeys, gp)
                    yield from fm_chunk_g((wub, "wsc_u"), c, T, 5, fT, fkeys, gp)
                    tmp, tk = (t4[:, 0:T], "t4") if cc % 2 == 0 else (fbuf[:].bitcast(F32)[:, 0:T], "fbuf")
                    S.op("act", [], [("pf", 4), tk], lambda e, tmp=tmp: e.activation(out=tmp, in_=psF[:, 4, 0:T], func=AF.Tanh, scale=0.5))
                    S.op("dve", [tk], [("pf", 4), tk],
                         lambda e, tmp=tmp: e.scalar_tensor_tensor(out=tmp, in0=tmp, scalar=1.0, in1=psF[:, 4, 0:T], op0=ALU.add, op1=ALU.mult))
                    S.op("dve", [tk], [("pf", 5), ("aT", cc)],
                         lambda e, cc=cc, tmp=tmp: e.scalar_tensor_tensor(out=aT[:, cc, 0:T], in0=tmp, scalar=0.5, in1=psF[:, 5, 0:T], op0=ALU.mult, op1=ALU.mult))
                    yield True
                for pr in pairs:
                    ptl = [tl[t] for t in pr]
                    for half in range(2):
                        yield from tm_pass((wdb, "wsc_d"), [r * 11 + k for k in range(11)], half, lambda k, P, c0: aT[:, k, c0:c0 + P],
                                           [("aT", k) for k in range(11)], ptl, [4, 5], dp)
                        for j, t in enumerate(pr):
                            resid_add(slots[t], ptl[j][0], 4 + j, half)
                        yield True
            for t, td in enumerate(tiles):
                P = td["P"]
                xt = xh[slots[t]]
                sy, syk = sumsq(xt[0:P, :], [("xh", slots[t])], P, D)
                ry, ryk = rstd(sy, syk, D, P)
                S.op("dve", [ryk, "bc_t"], [("xh", slots[t])],
                     lambda e, P=P, xt=xt, ry=ry: e.scalar_tensor_tensor(out=xt[0:P, :], in0=xt[0:P, :], scalar=ry[0:P], in1=bc_t[0:P, B_GFIN:B_GFIN + D], op0=ALU.mult, op1=ALU.mult))
                S.dma("pool", td["y_dst"], xt[0:P, :], [("xh", slots[t])], [], is_out=True)
                yield True

        QTs = [sb(f"QT{i}", [128, 8, 128], BF16) for i in range(4)]
        print("SBUF bytes remaining", nc.sbuf_bytes_remaining)

        blocks = []
        nblk = 4 if not dbg else (dbg if dbg > 0 else 0)
        for blk in range(nblk):
            tiles = []
            for j in range(4):
                i = blk * 4 + j
                td = dict(P=128, col0=j * 128, y_dst=y_p[i * 128:(i + 1) * 128, :])
                td["x_ap"] = xp[i * 128:(i + 1) * 128, :]
                td["front"] = (lambda slot, i=i, j=j: front(slot, 128, j * 128, i,
                                                      kv_p[i * 128:(i + 1) * 128, :], kr_p[i * 128:(i + 1) * 128, :],
                                                      KT[0:96, :, i * 128:(i + 1) * 128], VA[:, i, :, 0:64], [("KT", i)], [("VA", i)],
                                                      QTs[j], ("QT", j)))
                td["attn"] = (lambda tb, i=i, j=j: attn_prompt(i, tb, QTs[j], ("QT", j)))
                td["nops"] = 2 + 16 * ((i + 4) // 4)
                tiles.append(td)
            blocks.append((tiles, 512, blk, True))
        if do_sample:
            tiles = []
            for sq in range(2):
                td = dict(P=16, col0=sq * 16, y_dst=y_s[sq * 16:(sq + 1) * 16, :])
                td["x_ap"] = xs[sq * 16:(sq + 1) * 16, :]
                td["front"] = (lambda slot, sq=sq: front(slot, 16, sq * 16, 16,
                                                   kv_s[sq * 16:(sq + 1) * 16, :], kr_s[sq * 16:(sq + 1) * 16, :],
                                                   KTn[sq][0:96, :, 0:16], VAn[sq][0:16, :, 0:64], [("KTn", sq)], [("VAn", sq)],
                                                   QTs[sq], ("QT", sq)))
                td["attn"] = (lambda tb, sq=sq: attn_sample(sq, tb, QTs[sq], ("QT", sq)))
                td["nops"] = 2 + 33 * 6
                tiles.append(td)
            blocks.append((tiles, 32, 4, False))

        samp_keys = [(n, i) for n in ("KTs", "VAs", "cbuf", "krc") for i in range(4)] + \
                    [(n, i) for n in ("KTn", "VAn", "ckvT2", "Ktok2") for i in range(2)] + ["uTs"]
        pending = None
        pending_units = 0
        for (tiles, T, blk, is_prompt) in blocks:
            if not is_prompt:
                S.op("pool", [], samp_keys + [("KT", i) for i in range(NTILE)], lambda e: e.memset(scr[:], 0.0))
                for i in range(4):
                    S.op("pool", [], [("VAs", i)], lambda e, i=i: e.memset(VAs[i], 1.0))
                for i in range(2):
                    S.op("pool", [], [("VAn", i)], lambda e, i=i: e.memset(VAn[i], 1.0))
            S.hook = pending
            S.hook_steps = 0
            S.hook_acc = 0.0
            S.hook_total = pending_units
            slots = block_x(tiles, T, blk, is_prompt)
            S.hook = None
            if pending is not None:
                for _ in pending:
                    pass
            last_blk = (tiles is blocks[-1][0]) and not is_prompt
            if last_blk:
                S.op("pool", [], samp_keys + [("wgS", i) for i in range(len(wgS))] + [("wdS", i) for i in range(len(wdS))],
                     lambda e: e.memset(scr[:], 0.0))
            pending = gen_ffn(tiles, slots, T, deep=last_blk)
            npairs = (len(tiles) + 1) // 2
            pending_units = 2 * len(tiles) + NFF * 5 + 2 * npairs * 2 * 12
        if pending is not None:
            for _ in pending:
                pass
        S.finish()
    return nc


def _consts():
    ident = np.eye(128, dtype=np.float32).astype(ml_dtypes.bfloat16)
    inv = (1.0 / (np.float32(10000.0) ** (np.arange(0, 32, 2, dtype=np.float32) / np.float32(32)))).astype(np.float32)
    pos = np.zeros((128, 17), np.float32)
    for i in range(16):
        pos[:, i] = i * 128 + np.arange(128)
    pos[:, 16] = PAST + (np.arange(128) % 16)
    return ident, inv, pos


def _tile_cols(w):
    C = w.shape[1]
    return np.ascontiguousarray(w.reshape(8, 128, C // 128, 128).transpose(2, 1, 0, 3))


def kernel(x_prompt, x_sample, cache_kv_latent, cache_k_rope, state_conv,
           ln_mix, w_in, g_q, w_uq, g_kv, w_uk, w_uv, w_dw, b_dw, g_cn, b_cn, g_om, g_oc,
           w_out, ln_ffn, w_gate, w_up, w_down, g_final, _dbg=False):
    f = lambda a: np.ascontiguousarray(np.asarray(a, dtype=np.float32))
    ident, inv, pos = _consts()
    smallp = np.zeros((128, C_END), np.float32)
    smallp[:, C_LNMIX:C_LNMIX + 8] = f(ln_mix)[0].reshape(8, 128).T
    smallp[:, C_GQ:C_GQ + 3] = f(g_q)[0].reshape(3, 128).T
    smallp[:, C_GMIX:C_GMIX + 4] = f(g_om)[0].reshape(4, 128).T
    smallp[:, C_GMIX + 4:C_GMIX + 8] = f(g_oc)[0].reshape(4, 128).T
    smallp[:, C_LNFFN:C_LNFFN + 8] = f(ln_ffn)[0].reshape(8, 128).T
    smallp[:, C_BDW:C_BDW + 4] = f(b_dw)[0].reshape(4, 128).T
    smallp[:, C_WDW:C_WDW + 124] = f(w_dw)[0].reshape(31, 4, 128).transpose(2, 1, 0).reshape(128, 124)
    bcp = np.zeros((1, B_END), np.float32)
    bcp[0, B_GKV:B_GKV + 256] = f(g_kv)[0]
    bcp[0, B_GCN:B_GCN + 512] = f(g_cn)[0]
    bcp[0, B_BCN:B_BCN + 512] = f(b_cn)[0]
    bcp[0, B_GFIN:B_GFIN + 1024] = f(g_final)
    bcp[0, B_INV:B_INV + 16] = inv
    w_in0 = f(w_in)[0]
    shared = dict(
        w_in=np.ascontiguousarray(w_in0[:, 0:672]), wc_r=_tile_cols(w_in0[:, 672:1696]),
        w_uq=f(w_uq)[0], w_uk=f(w_uk)[0].reshape(256, 512), w_uv=f(w_uv)[0].reshape(256, 512),
        w_out=f(w_out)[0], wg_r=_tile_cols(f(w_gate)[0]), wu_r=_tile_cols(f(w_up)[0]), w_down=f(w_down)[0],
        smallp=smallp, bcp=bcp, postab=pos, identd=ident)
    xp_, xs_ = f(x_prompt), f(x_sample)
    ckv_, ckr_, cst_ = f(cache_kv_latent)[0], f(cache_k_rope)[0], f(state_conv)[0]
    in_maps = []
    for c in range(N_CORES):
        m = dict(shared)
        m["xp"] = xp_[c]
        m["xs"] = xs_[2 * c:2 * c + 2].reshape(32, D)
        m["ckv_c"] = ckv_[2 * c:2 * c + 2]
        m["ckr_c"] = ckr_[2 * c:2 * c + 2]
        m["cst"] = cst_[2 * c:2 * c + 2]
        in_maps.append(m)
    nc = build_program(dbg=_dbg)
    res = run_bass_kernel_spmd(nc, in_maps, core_ids=list(range(N_CORES)))
    R = res.results
    y_prompt = np.stack([R[c]["y_p"] for c in range(N_CORES)])
    y_sample = np.concatenate([R[c]["y_s"].reshape(2, 16, D) for c in range(N_CORES)])
    kvp = np.stack([R[c]["kv_p"] for c in range(N_CORES)])[None]
    krp = np.stack([R[c]["kr_p"] for c in range(N_CORES)])[None]
    cvp = np.stack([R[c]["cv_p"] for c in range(N_CORES)])[None]
    kvs = np.concatenate([R[c]["kv_s"].reshape(2, 16, 256) for c in range(N_CORES)])[None]
    krs = np.concatenate([R[c]["kr_s"].reshape(2, 16, 32) for c in range(N_CORES)])[None]
    cvs = np.concatenate([R[c]["cv_s"] for c in range(N_CORES)])[None]
    return tuple(np.ascontiguousarray(a.astype(np.float32)) for a in (y_prompt, y_sample, kvp, krp, cvp, kvs, krs, cvs))
```

```python
import math
from contextlib import ExitStack

import numpy as np
import ml_dtypes

import concourse.bass as bass
import concourse.mybir as mybir
from concourse.bass_utils import run_bass_kernel_spmd

F32, BF16, I32 = mybir.dt.float32, mybir.dt.bfloat16, mybir.dt.int32
AF, ALU = mybir.ActivationFunctionType, mybir.AluOpType

EPS = 1e-6
D = 1024
SEQ = 2048
NTILE = 16
DFF = 2816
NFF = 22
PAST = 4096
SCALE = 1.0 / math.sqrt(96.0)
N_CORES = 8
NDMA = 40
NDMA_SP = 28

C_LNMIX, C_GQ, C_GMIX, C_LNFFN, C_BDW, C_WDW, C_END = 0, 8, 11, 19, 27, 31, 155
B_GKV, B_GCN, B_BCN, B_GFIN, B_INV, B_END = 0, 256, 768, 1280, 2304, 2320


class Op:
    __slots__ = ("key", "val", "clk")

    def __init__(self, key, val, clk):
        self.key, self.val, self.clk = key, val, clk


class Sched:
    def __init__(self, nc, es):
        self.nc = nc
        self.E = {"pe": nc.tensor, "act": nc.scalar, "dve": nc.vector, "pool": nc.gpsimd, "sp": nc.sync}
        self.sem = {k: es.enter_context(nc.semaphore("sem_" + k)) for k in self.E}
        self.cnt = {k: 0 for k in self.E}
        self.clk = {k: {} for k in self.E}
        self.lastw, self.readers = {}, {}
        self.dsem = [es.enter_context(nc.semaphore(f"dsem{i}")) for i in range(NDMA)]
        self.dcnt = [0] * NDMA
        self.dlast = [None] * NDMA
        self.dnext = 0
        self.dnext_sw = 0
        self.out_ops = []
        self.hook = None
        self.hook_safe = True
        self.hook_rate = 1.0
        self.hook2 = None
        self._in_hook2 = False
        self.hook_acc = 0.0
        self.hook_steps = 0
        self.wkeys = {}
        self.hook_on = False
        self._in_hook = False

    def _semh(self, key):
        return self.sem[key] if isinstance(key, str) else self.dsem[key[1]]

    def _wait(self, eng, op):
        if op is None:
            return
        c = self.clk[eng]
        if c.get(op.key, 0) >= op.val:
            return
        if op.key == "pe" and eng == "pe":
            return
        self.E[eng].wait_ge(self._semh(op.key), op.val)
        for k, v in op.clk.items():
            if c.get(k, 0) < v:
                c[k] = v
        c[op.key] = op.val

    def _deps(self, reads, writes):
        deps = []
        for r in reads:
            w = self.lastw.get(r)
            if w is not None:
                deps.append(w)
        for k in writes:
            w = self.lastw.get(k)
            if w is not None:
                deps.append(w)
            deps.extend(self.readers.get(k, ()))
        return deps

    def _commit(self, op, reads, writes):
        for r in reads:
            self.readers.setdefault(r, []).append(op)
        for k in writes:
            self.lastw[k] = op
            self.readers[k] = []

    def op(self, eng, reads, writes, fn):
        for d in self._deps(reads, writes):
            self._wait(eng, d)
        inst = fn(self.E[eng])
        self.cnt[eng] += 1
        inst.then_inc(self.sem[eng], 1)
        o = Op(eng, self.cnt[eng], dict(self.clk[eng]))
        self._commit(o, reads, writes)
        if eng == "pe" and self.hook is not None and self.hook_on and not self._in_hook and not self._in_hook2:
            self.hook_acc += self.hook_rate
            while self.hook_acc >= 1.0:
                self.hook_acc -= 1.0
                self.step_hook()
        if eng == "pe" and self.hook2 is not None and not self._in_hook2 and not self._in_hook:
            self._in_hook2 = True
            try:
                next(self.hook2, None)
            finally:
                self._in_hook2 = False
        return o

    def step_hook(self):
        self._in_hook = True
        try:
            v = next(self.hook, None)
            self.hook_steps += 1
            self.hook_safe = True if v is None else bool(v)
        finally:
            self._in_hook = False

    def run_side(self, gen):
        self._in_hook2 = True
        try:
            for _ in gen:
                pass
        finally:
            self._in_hook2 = False

    def drain_to_safe(self):
        while self.hook is not None and not self.hook_safe:
            self.step_hook()

    def dma(self, q, out, in_, reads, writes, is_out=False, slow=False):
        if q == "pool":
            i = NDMA_SP + self.dnext_sw
            self.dnext_sw = (self.dnext_sw + 1) % (NDMA - NDMA_SP)
        else:
            i = self.dnext
            self.dnext = (i + 1) % NDMA_SP
        self._wait(q, self.dlast[i])
        for d in self._deps(reads, writes):
            self._wait(q, d)
        self.dcnt[i] += 16
        if slow:
            self.E[q].dma_start(out=out, in_=in_, allow_slow_non_contiguous=True).then_inc(self.dsem[i], 16)
        else:
            self.E[q].dma_start(out=out, in_=in_).then_inc(self.dsem[i], 16)
        o = Op(("d", i), self.dcnt[i], dict(self.clk[q]))
        self.dlast[i] = o
        self._commit(o, reads, writes)
        if is_out:
            self.out_ops.append(o)
        return o

    def finish(self):
        for o in self.out_ops:
            self._wait("sp", o)
        for e in ("pe", "act", "dve", "pool"):
            if self.cnt[e] > 0:
                self._wait("sp", Op(e, self.cnt[e], {}))


def bc_ins(a, pos, n):
    apl = [list(x) for x in a.ap]
    apl.insert(pos, [0, n])
    return bass.AP(a.tensor, a.offset, apl)


def build_program(do_sample=True, dbg=False):
    nc = bass.Bass("TRN2", target_bir_lowering=False)

    def din(name, shape, dt=F32):
        return nc.dram_tensor(name, list(shape), dt, kind="ExternalInput").ap()

    def dout(name, shape):
        return nc.dram_tensor(name, list(shape), F32, kind="ExternalOutput").ap()

    xp = din("xp", [SEQ, D])
    xs = din("xs", [32, D])
    ckv_c = din("ckv_c", [2, PAST, 256])
    ckr_c = din("ckr_c", [2, PAST, 32])
    cst = din("cst", [2, 30, 512])
    w_in = din("w_in", [D, 672])
    wc_r = din("wc_r", [8, 128, 8, 128])
    w_uq = din("w_uq", [384, 768])
    w_uk = din("w_uk", [256, 512])
    w_uv = din("w_uv", [256, 512])
    w_out = din("w_out", [D, D])
    wg_r = din("wg_r", [NFF, 128, 8, 128])
    wu_r = din("wu_r", [NFF, 128, 8, 128])
    w_down = din("w_down", [DFF, D])
    smallp = din("smallp", [128, C_END])
    bcp = din("bcp", [1, B_END])
    postab = din("postab", [128, 17])
    identd = din("identd", [128, 128], BF16)

    def dscr(name, shape):
        return nc.dram_tensor(name, list(shape), BF16).ap()

    wgb = dscr("wgb", [NFF * 128, 1024])
    wub = dscr("wub", [NFF * 128, 1024])
    wcb = dscr("wcb", [8 * 128, 1024])
    wob = dscr("wob", [D, D])
    wdb = dscr("wdb", [DFF, D])

    y_p = dout("y_p", [SEQ, D])
    y_s = dout("y_s", [32, D])
    kv_p = dout("kv_p", [SEQ, 256])
    kr_p = dout("kr_p", [SEQ, 32])
    cv_p = dout("cv_p", [30, 512])
    kv_s = dout("kv_s", [32, 256])
    kr_s = dout("kr_s", [32, 32])
    cv_s = dout("cv_s", [2, 30, 512])

    with ExitStack() as es:
        S = Sched(nc, es)

        def sb(name, shape, dt=F32):
            return es.enter_context(nc.sbuf_tensor(name, list(shape), dt))

        psF = es.enter_context(nc.psum_tensor("psF", [128, 6, 512], F32))
        psB = es.enter_context(nc.psum_tensor("psB", [128, 2, 1024], BF16))

        ident = sb("ident", [128, 128], BF16)
        sp_t = sb("sp_t", [128, C_END])
        bc_t = sb("bc_t", [128, B_END])
        pos_t = sb("pos_t", [128, 17])
        cos_t = sb("cos_t", [128, 17, 16])
        sin_t = sb("sin_t", [128, 17, 16])
        mhalf = sb("mhalf", [128, 1])
        scr = sb("scr", [128, 1])
        stat = sb("stat", [128, 512])
        w_in_a = sb("w_in_a", [128, 8, 672], BF16)
        w_uq_t = sb("w_uq_t", [128, 3, 768], BF16)
        w_uk_t = sb("w_uk_t", [128, 2, 512], BF16)
        w_uv_t = sb("w_uv_t", [128, 2, 512], BF16)
        KT = sb("KT", [128, 8, SEQ], BF16)
        VA = sb("VA", [128, NTILE, 8, 65], BF16)
        diag = sb("diag", [128, 31, 128], BF16)
        uTp = sb("uTp", [128, 4, 542], BF16)
        NXH = 8
        xh = [sb(f"xh{i}", [128, D]) for i in range(NXH)]
        tokb = [sb(f"tokb{i}", [128, D], BF16) for i in range(2)]
        actT = sb("actT", [128, 8, 512], BF16)
        aT = sb("aT", [128, 11, 512], BF16)
        fT = sb("fT", [128, 8, 512], BF16)
        fbuf = sb("fbuf", [128, D], BF16)
        t4 = sb("t4", [128, 512])
        wg = [sb(f"wg{i}", [128, 8, 128], BF16) for i in range(4)]
        wd = [sb(f"wd{i}", [128, 512], BF16) for i in range(5)]
        cqn = sb("cqn", [128, 384], BF16)
        cqT = sb("cqT", [128, 3, 128], BF16)
        Qtok = sb("Qtok", [128, 8, 96], BF16)
        ropec = sb("ropec", [128, 8, 2, 16])
        ropes = sb("ropes", [128, 8, 2, 16])
        ckvf = [sb(f"ckvf{i}", [128, 256]) for i in range(2)]
        ckvb = sb("ckvb", [128, 256], BF16)
        ckvT = sb("ckvT", [128, 2, 128], BF16)
        krf = [sb(f"krf{i}", [128, 32]) for i in range(2)]
        Ktok = sb("Ktok", [128, 8, 96], BF16)
        PT = [sb(f"PT{i}", [128, 4, 128], BF16) for i in range(2)]
        otok = sb("otok", [128, 512])
        rec = sb("rec", [128, 8])
        dwT = sb("dwT", [128, 4, 512], BF16)
        t1 = sb("t1", [128, 512])
        t2 = sb("t2", [128, 512])
        junk = sb("junk", [128, D], BF16)
        utok = sb("utok", [32, 512])
        KTs = [KT[:, b // 2, (b % 2) * 1024:(b % 2 + 1) * 1024].rearrange("p (h t) -> p h t", h=8) for b in range(4)]
        VAs = [KT[:, 2, b * 520:(b + 1) * 520].rearrange("p (h d) -> p h d", h=8) for b in range(3)] + \
              [KT[:, 3, 0:520].rearrange("p (h d) -> p h d", h=8)]
        VAn = [KT[:, 3, (1 + b) * 520:(2 + b) * 520].rearrange("p (h d) -> p h d", h=8) for b in range(2)]
        cbuf = [KT[:, 4, b * 256:(b + 1) * 256] for b in range(4)]
        ckvT2 = [KT[:, 4, 1024 + b * 256:1024 + (b + 1) * 256].rearrange("p (k t) -> p k t", k=2) for b in range(2)]
        krc = [KT[:, 5, b * 64:(b + 1) * 64].bitcast(F32) for b in range(4)]
        KTn = [KT[:, 6, b * 128:(b + 1) * 128].rearrange("p (h t) -> p h t", h=8) for b in range(2)]
        Ktok2 = [KT[:, 6, 256 + b * 768:256 + (b + 1) * 768].rearrange("p (h d) -> p h d", h=8) for b in range(2)]
        uTs = KT[:, 5, 1024:1024 + 368].rearrange("p (c s t) -> p c s t", c=4, s=2)

        wgS = [KT[:, r, c * 1024:(c + 1) * 1024].rearrange("p (k j) -> p k j", k=8) for r in range(6) for c in range(2)]
        wdS = [KT[:, 6 + b // 4, (b % 4) * 512:(b % 4 + 1) * 512] for b in range(8)]
        pools = {"wg": ([w_[:] for w_ in wg], [0], "wg"), "wgS": (wgS, [0], "wgS"), "wd": ([w_[:] for w_ in wd], [0], "wd"), "wdS": (wdS, [0], "wdS")}

        stat_i = [0]

        def newstat():
            i = stat_i[0] % 512
            stat_i[0] += 1
            return stat[:, i:i + 1], ("st", i)

        def rstd(ssq, ssq_k, n, P=128):
            t, tk = newstat()
            r, rk = newstat()
            S.op("dve", [ssq_k], [tk], lambda e: e.tensor_scalar(t[0:P], ssq[0:P], 1.0 / n, EPS, ALU.mult, ALU.add))
            S.op("pool", [tk, "mhalf"], [rk], lambda e: e.tensor_tensor(r[0:P], t[0:P], mhalf[0:P], ALU.pow))
            return r, rk

        def sumsq(src, src_keys, P, n, scale=1.0, excl=()):
            s, sk = newstat()
            S.op("act", list(src_keys), [sk, "junk"] + list(excl),
                 lambda e: e.activation(out=junk[0:P, 0:n], in_=src, func=AF.Square, scale=scale, accum_out=s[0:P]))
            return s, sk

        def transposes(src_fn, nk, P, width, dst, dst_keys, src_keys, bank, gain=None):
            pk = ("pb", bank)

            def f(e):
                last = None
                for k in range(nk):
                    last = e.transpose(psB[0:width, bank, k * P:(k + 1) * P], src_fn(k), ident[0:P, 0:P])
                return last
            S.op("pe", list(src_keys) + ["ident"], [pk], f)
            src = psB[0:width, bank, 0:nk * P].rearrange("p (k t) -> p k t", k=nk)
            if gain is None:
                S.op("act", [], [pk] + list(dst_keys), lambda e: e.copy(out=dst, in_=src))
            else:
                g = bc_ins(gain, 2, P)
                S.op("dve", ["sp_t"], [pk] + list(dst_keys), lambda e: e.tensor_tensor(out=dst, in0=src, in1=g, op=ALU.mult))

        S.dma("sp", ident[:], identd, [], ["ident"])
        S.dma("sp", sp_t[:], smallp, [], ["sp_t"])
        S.dma("sp", pos_t[:], postab, [], ["pos_t"])
        S.dma("sp", bc_t[:], bass.AP(bcp.tensor, 0, [[0, 128], [1, B_END]]), [], ["bc_t"])
        S.dma("pool", w_in_a[:], w_in.rearrange("(k p) c -> p k c", p=128), [], ["w_in_a"])
        S.dma("pool", w_uq_t[:], w_uq.rearrange("(k p) c -> p k c", p=128), [], ["w_uq"])
        S.dma("pool", w_uk_t[:], w_uk.rearrange("(k p) c -> p k c", p=128), [], ["w_uk"])
        S.dma("pool", w_uv_t[:], w_uv.rearrange("(k p) c -> p k c", p=128), [], ["w_uv"])
        def convert(items):
            for dst, src, key, rows in items:
                step = 512 if rows == 1024 else 704
                for r0 in range(0, rows, step):
                    S.dma("pool", dst[r0:r0 + step, :], src[r0:r0 + step, :], [], [(key, r0)])
                S.wkeys[key] = [(key, r0) for r0 in range(0, rows, step)]

        conv_late = [(wob, w_out, "wsc_o", 1024),
                     (wgb, wg_r.rearrange("c p k j -> (c p) (k j)"), "wsc_g", NFF * 128),
                     (wub, wu_r.rearrange("c p k j -> (c p) (k j)"), "wsc_u", NFF * 128),
                     (wdb, w_down, "wsc_d", DFF)]
        late_done = [False]
        S.op("dve", [], ["mhalf"], lambda e: e.memset(mhalf[:], -0.5))
        S.op("pool", [], [("VA", i) for i in range(NTILE)], lambda e: e.memset(VA[:, :, :, 64:65], 1.0))
        convert([(wcb, wc_r.rearrange("c p k j -> (c p) (k j)"), "wsc_c", 1024)])
        S.op("pool", [], ["uTp"], lambda e: e.memset(uTp[:], 0.0))
        S.op("dve", ["sp_t"], ["sp_t"],
             lambda e: e.tensor_scalar(sp_t[:, C_GMIX + 4:C_GMIX + 8], sp_t[:, C_GMIX + 4:C_GMIX + 8], 0.5, None, ALU.mult))

        TWO_PI = 2.0 * math.pi
        C1 = 6.28125
        C2 = TWO_PI - C1
        ang = t1[:, 0:272].rearrange("p (a b) -> p a b", a=17)
        wk = t2[:, 0:272].rearrange("p (a b) -> p a b", a=17)
        wk2 = t4[:, 0:272].rearrange("p (a b) -> p a b", a=17)
        ki = xh[0][:, 0:272].bitcast(I32).rearrange("p (a b) -> p a b", a=17)
        inv_b = bc_ins(bc_t[:, B_INV:B_INV + 16], 1, 17)
        pos_b = bc_ins(pos_t[:, 0:17], 2, 16)
        S.op("dve", ["bc_t", "pos_t"], ["t1"], lambda e: e.tensor_tensor(out=ang, in0=pos_b, in1=inv_b, op=ALU.mult))
        for tab, shift in ((sin_t, 0.0), (cos_t, math.pi / 2)):
            S.op("dve", ["t1"], ["t2"], lambda e: e.tensor_scalar(wk, ang, shift, 1.0 / TWO_PI, ALU.add, ALU.mult))
            S.op("dve", ["t2"], [("xh", 0)], lambda e: e.tensor_copy(out=ki, in_=wk))
            S.op("dve", [("xh", 0)], ["t2"], lambda e: e.tensor_copy(out=wk, in_=ki))
            S.op("dve", ["t2", "t1"], ["t4"], lambda e: e.scalar_tensor_tensor(out=wk2, in0=wk, scalar=-C1, in1=ang, op0=ALU.mult, op1=ALU.add))
            S.op("dve", ["t4", "t2"], ["t4"], lambda e: e.scalar_tensor_tensor(out=wk2, in0=wk, scalar=-C2, in1=wk2, op0=ALU.mult, op1=ALU.add))
            if shift != 0.0:
                S.op("dve", ["t4"], ["t4"], lambda e: e.tensor_scalar(wk2, wk2, shift, None, ALU.add))
            S.op("dve", ["t4"], ["t2"], lambda e: e.tensor_scalar(wk, wk2, math.pi, -TWO_PI, ALU.is_gt, ALU.mult))
            S.op("dve", ["t2", "t4"], ["t4"], lambda e: e.tensor_tensor(out=wk2, in0=wk2, in1=wk, op=ALU.add))
            S.op("dve", ["t4"], ["t2"], lambda e: e.tensor_scalar(wk, wk2, -math.pi, TWO_PI, ALU.is_lt, ALU.mult))
            S.op("dve", ["t2", "t4"], ["t4"], lambda e: e.tensor_tensor(out=wk2, in0=wk2, in1=wk, op=ALU.add))
            S.op("act", ["t4"], [("tab", id(tab))], lambda e, tab=tab: e.activation(out=tab[:], in_=wk2, func=AF.Sin))
        TABK = [("tab", id(sin_t)), ("tab", id(cos_t))]

        def rope(src, P, ti, out_lo, out_hi, nh, keys_r, keys_w):
            cb = bc_ins(bc_ins(cos_t[0:P, ti, :], 1, 2), 1, nh)
            sbb = bc_ins(bc_ins(sin_t[0:P, ti, :], 1, 2), 1, nh)
            rc = ropec[0:P, 0:nh]
            rs = ropes[0:P, 0:nh]
            S.op("dve", TABK, list(keys_r) + ["ropec"], lambda e: e.tensor_tensor(out=rc, in0=src, in1=cb, op=ALU.mult))
            S.op("dve", TABK, list(keys_r) + ["ropes"], lambda e: e.tensor_tensor(out=rs, in0=src, in1=sbb, op=ALU.mult))
            S.op("dve", ["ropec", "ropes"], list(keys_w),
                 lambda e: e.tensor_tensor(out=out_lo, in0=ropec[0:P, 0:nh, 0, :], in1=ropes[0:P, 0:nh, 1, :], op=ALU.subtract))
            S.op("dve", ["ropec", "ropes"], list(keys_w),
                 lambda e: e.tensor_tensor(out=out_hi, in0=ropec[0:P, 0:nh, 1, :], in1=ropes[0:P, 0:nh, 0, :], op=ALU.add))

        xh_i = [0]

        def load_x(x_ap, P):
            s_ = xh_i[0] % NXH
            xh_i[0] += 1
            S.dma("sp", xh[s_][0:P, :], x_ap, [], [("xh", s_)])
            return s_
        tokb_i = [0]

        def front(x_src, P, col0, ti, kv_dst, kr_dst, kt_dst, va_dst, kt_keys, va_keys, qt, qtk):
            s = x_src
            xk = ("xh", s)
            xt = xh[s]
            ssq, ssqk = sumsq(xt[0:P, :], [xk], P, D)
            r, rk = rstd(ssq, ssqk, D, P)
            tb = tokb_i[0] % 2
            tokb_i[0] += 1
            hn = tokb[tb]
            S.op("act", [xk, rk], [("tokb", tb, 0), ("tokb", tb, 1)], lambda e: e.activation(out=hn[0:P, :], in_=xt[0:P, :], func=AF.Copy, scale=r[0:P]))
            transposes(lambda k: hn[0:P, k * 128:(k + 1) * 128], 8, P, 128, actT[:, :, col0:col0 + P], [("actT", col0)],
                       [("tokb", tb, 0), ("tokb", tb, 1)], 0, gain=sp_t[:, C_LNMIX:C_LNMIX + 8])
            def fa(e):
                last = None
                for k in range(8):
                    e.matmul(psF[0:P, 0, 0:384], lhsT=actT[:, k, col0:col0 + P], rhs=w_in_a[:, k, 0:384], start=(k == 0), stop=(k == 7))
                for k in range(8):
                    last = e.matmul(psF[0:P, 1, 0:288], lhsT=actT[:, k, col0:col0 + P], rhs=w_in_a[:, k, 384:672], start=(k == 0), stop=(k == 7))
                return last
            S.op("pe", [("actT", col0), "w_in_a"], [("pf", 0), ("pf", 1)], fa)
            sq, sqk = sumsq(psF[0:P, 0, 0:384], [], P, 384, excl=[("pf", 0)])
            rq, rqk = rstd(sq, sqk, 384, P)
            S.op("act", [rqk], [("pf", 0), "cqn"], lambda e: e.activation(out=cqn[0:P, :], in_=psF[0:P, 0, 0:384], func=AF.Copy, scale=rq[0:P]))
            transposes(lambda k: cqn[0:P, k * 128:(k + 1) * 128], 3, P, 128, cqT[:, :, 0:P], ["cqT"], ["cqn"], 0,
                       gain=sp_t[:, C_GQ:C_GQ + 3])
            sk_, skk = sumsq(psF[0:P, 1, 0:256], [], P, 256, excl=[("pf", 1)])
            rkv, rkvk = rstd(sk_, skk, 256, P)
            cb_i = ti % 2
            cf = ckvf[cb_i]
            S.op("dve", [rkvk, "bc_t"], [("pf", 1), ("ckvf", cb_i)],
                 lambda e: e.scalar_tensor_tensor(out=cf[0:P, :], in0=psF[0:P, 1, 0:256], scalar=rkv[0:P], in1=bc_t[0:P, B_GKV:B_GKV + 256], op0=ALU.mult, op1=ALU.mult))
            S.op("act", [("ckvf", cb_i)], ["ckvb"], lambda e: e.copy(out=ckvb[0:P, :], in_=cf[0:P, :]))
            kf = krf[cb_i]
            ksrc = psF[0:P, 1, 256:288].rearrange("p (a h d) -> p a h d", a=1, h=2)
            rope(ksrc, P, ti, kf[0:P, 0:16].rearrange("p (a d) -> p a d", a=1), kf[0:P, 16:32].rearrange("p (a d) -> p a d", a=1), 1,
                 [("pf", 1)], [("krf", cb_i)])
            def fq(e):
                last = None
                for hb in range(2):
                    for h4 in range(4):
                        hh = hb * 4 + h4
                        for k in range(3):
                            last = e.matmul(psF[0:P, hb, h4 * 128:h4 * 128 + 96], lhsT=cqT[:, k, 0:P], rhs=w_uq_t[:, k, hh * 96:(hh + 1) * 96],
                                            start=(k == 0), stop=(k == 2))
                return last
            S.op("pe", ["cqT", "w_uq"], [("pf", 0), ("pf", 1)], fq)
            qv = psF[0:P, 0:2, :].rearrange("p b (h d) -> p (b h) d", h=4)
            S.op("act", [], [("pf", 0), ("pf", 1), "Qtok"], lambda e: e.copy(out=Qtok[0:P, :, 0:64], in_=qv[:, :, 0:64]))
            qpe = qv[:, :, 64:96].rearrange("p h (a d) -> p h a d", a=2)
            rope(qpe, P, ti, Qtok[0:P, :, 64:80], Qtok[0:P, :, 80:96], 8, [("pf", 0), ("pf", 1)], ["Qtok"])
            transposes(lambda h: Qtok[0:P, h, :], 8, P, 96, qt[0:96, :, 0:P], [qtk], ["Qtok"], 0)
            transposes(lambda k: ckvb[0:P, k * 128:(k + 1) * 128], 2, P, 128, ckvT[:, :, 0:P], ["ckvT"], ["ckvb"], 0)
            kv_from_ckvT(P, kf, ("krf", cb_i), kt_dst, va_dst, kt_keys, va_keys, 0, 1)
            S.dma("pool", kv_dst, cf[0:P, :], [("ckvf", cb_i)], [], is_out=True)
            S.dma("pool", kr_dst, kf[0:P, :], [("krf", cb_i)], [], is_out=True)
            return s

        def kv_from_ckvT(P, kf, kfk, kt_dst, va_dst, kt_keys, va_keys, bk, bv):
            def fk(e):
                last = None
                for k in range(2):
                    e.matmul(psF[0:P, bk, :], lhsT=ckvT[:, k, 0:P], rhs=w_uk_t[:, k, :], start=(k == 0), stop=(k == 1))
                for k in range(2):
                    last = e.matmul(psF[0:P, bv, :], lhsT=ckvT[:, k, 0:P], rhs=w_uv_t[:, k, :], start=(k == 0), stop=(k == 1))
                return last
            S.op("pe", ["ckvT", "w_uk", "w_uv"], [("pf", bk), ("pf", bv)], fk)
            S.op("act", [], [("pf", bk), "Ktok"], lambda e: e.copy(out=Ktok[0:P, :, 0:64], in_=psF[0:P, bk, :].rearrange("p (h d) -> p h d", h=8)))
            S.op("dve", [kfk], ["Ktok"], lambda e: e.tensor_copy(out=Ktok[0:P, :, 64:96], in_=bc_ins(kf[0:P, :], 1, 8)))
            S.op("dve", [], [("pf", bv)] + list(va_keys), lambda e: e.tensor_copy(out=va_dst, in_=psF[0:P, bv, :].rearrange("p (h d) -> p h d", h=8)))
            transposes(lambda h: Ktok[0:P, h, :], 8, P, 96, kt_dst, kt_keys, ["Ktok"], 0)

        pt_i = [0]

        def attn_prompt(i, tb, qt, qtk):
            nk = i + 1
            glist = [(h, list(range(g, min(g + 4, nk)))) for h in range(8) for g in range(0, nk, 4)]

            def emit_s(idx):
                h, g = glist[idx]
                bb = idx % 2

                def fs(e):
                    last = None
                    for j, kt in enumerate(g):
                        last = e.matmul(psF[:, bb, j * 128:(j + 1) * 128], lhsT=KT[0:96, h, kt * 128:(kt + 1) * 128], rhs=qt[0:96, h, :], start=True, stop=True)
                    return last
                S.op("pe", [qtk] + [("KT", kt) for kt in g], [("pf", bb)], fs)
                n = len(g) * 128
                S.op("act", [], [("pf", bb), ("PT", bb)],
                     lambda e: e.activation(out=PT[bb][:].rearrange("p a b -> p (a b)")[:, 0:n], in_=psF[:, bb, 0:n], func=AF.Exp, scale=SCALE))
                if i in g:
                    j = g.index(i)
                    S.op("pool", [], [("PT", bb)], lambda e: e.memset(PT[bb][64:128, j, 0:64], 0.0))

            def emit_pv(idx):
                h, g = glist[idx]
                bb = idx % 2
                pob = 2 + h // 4
                pocol = (h % 4) * 128

                def fp(e):
                    last = None
                    for j, kt in enumerate(g):
                        last = e.matmul(psF[:, pob, pocol:pocol + 65], lhsT=PT[bb][:, j, :], rhs=VA[:, kt, h, :], start=(kt == 0), stop=(kt == nk - 1))
                    return last
                S.op("pe", [("PT", bb)] + [("VA", kt) for kt in g], [("pf", pob)], fp)

            for idx in range(len(glist)):
                emit_s(idx)
                if idx >= 1:
                    emit_pv(idx - 1)
            emit_pv(len(glist) - 1)
            pov = psF[:, 2:4, :].rearrange("p b (h d) -> p (b h) d", h=4)
            S.op("dve", [], [("pf", 2), ("pf", 3), "rec"], lambda e: e.reciprocal(out=rec[:].rearrange("p (h a) -> p h a", a=1), in_=pov[:, :, 64:65]))
            S.op("dve", ["rec"], [("pf", 2), ("pf", 3), "otok"],
                 lambda e: e.tensor_tensor(out=otok[:].rearrange("p (h d) -> p h d", h=8), in0=pov[:, :, 0:64], in1=bc_ins(rec[:], 2, 64), op=ALU.mult))
            so, sok = sumsq(otok[:], ["otok"], 128, 512)
            ro, rok = rstd(so, sok, 512)
            S.op("act", ["otok", rok], [("tokb", tb, 0)], lambda e: e.activation(out=tokb[tb][:, 0:512], in_=otok[:], func=AF.Copy, scale=ro[:]))


        def attn_sample(sq, tb, qt, qtk):
            NKT = PAST // 128
            first = [True, True]

            def st_load(kt):
                b = kt % 4
                S.dma("pool", cbuf[b], ckv_c[sq, kt * 128:(kt + 1) * 128, :], [], [("cbuf", b)])
                S.dma("pool", krc[b], ckr_c[sq, kt * 128:(kt + 1) * 128, :], [], [("krc", b)])

            def st_t1(kt):
                b, c2 = kt % 4, kt % 2
                transposes(lambda k: cbuf[b][:, k * 128:(k + 1) * 128], 2, 128, 128, ckvT2[c2], [("ckvT2", c2)], [("cbuf", b)], 0)

            def st_m(kt):
                b, c2 = kt % 4, kt % 2

                def fk(e):
                    last = None
                    for k in range(2):
                        e.matmul(psF[:, 0, :], lhsT=ckvT2[c2][:, k, :], rhs=w_uk_t[:, k, :], start=(k == 0), stop=(k == 1))
                    for k in range(2):
                        last = e.matmul(psF[:, 1, :], lhsT=ckvT2[c2][:, k, :], rhs=w_uv_t[:, k, :], start=(k == 0), stop=(k == 1))
                    return last
                S.op("pe", [("ckvT2", c2), "w_uk", "w_uv"], [("pf", 0), ("pf", 1)], fk)
                S.op("act", [], [("pf", 0), ("Ktok2", c2)], lambda e: e.copy(out=Ktok2[c2][:, :, 0:64], in_=psF[:, 0, :].rearrange("p (h d) -> p h d", h=8)))
                S.op("dve", [("krc", b)], [("Ktok2", c2)], lambda e: e.tensor_copy(out=Ktok2[c2][:, :, 64:96], in_=bc_ins(krc[b], 1, 8)))
                S.op("dve", [], [("pf", 1), ("VAs", b)], lambda e: e.tensor_copy(out=VAs[b][:, :, 0:64], in_=psF[:, 1, :].rearrange("p (h d) -> p h d", h=8)))

            def st_t2(kt):
                b, c2 = kt % 4, kt % 2
                transposes(lambda h: Ktok2[c2][:, h, :], 8, 128, 96, KTs[b][0:96, :, :], [("KTs", b)], [("Ktok2", c2)], 0)

            def bufs(kt):
                if kt == NKT:
                    return KTn[sq], ("KTn", sq), VAn[sq], ("VAn", sq), 16
                b = kt % 4
                return KTs[b], ("KTs", b), VAs[b], ("VAs", b), 128

            def st_b(kt):
                kt_t, kt_k, _, _, KP = bufs(kt)
                bb = kt % 2

                def fs(e):
                    last = None
                    for h in range(8):
                        last = e.matmul(psF[0:KP, bb, h * 16:(h + 1) * 16], lhsT=kt_t[0:96, h, 0:KP], rhs=qt[0:96, h, 0:16], start=True, stop=True)
                    return last
                S.op("pe", [qtk, kt_k], [("pf", bb)], fs)
                S.op("act", [], [("pf", bb), ("PT", bb)],
                     lambda e: e.activation(out=PT[bb][0:KP, 0, :], in_=psF[0:KP, bb, 0:128], func=AF.Exp, scale=SCALE))

            def st_c(kt):
                _, _, va_t, va_k, KP = bufs(kt)
                bb = kt % 2
                new = kt == NKT

                def fp(e):
                    last = None
                    for h in range(8):
                        st = first[h // 4]
                        first[h // 4] = False
                        last = e.matmul(psF[0:16, 2 + h // 4, (h % 4) * 128:(h % 4) * 128 + 65], lhsT=PT[bb][0:KP, 0, h * 16:(h + 1) * 16],
                                        rhs=va_t[0:KP, h, :], start=st, stop=new, skip_group_check=True)
                    return last
                S.op("pe", [("PT", bb), va_k], [("pf", 2), ("pf", 3)], fp)

            for step in range(NKT + 6):
                if 0 <= step - 5 <= NKT:
                    st_c(step - 5)
                if 0 <= step - 4 <= NKT:
                    st_b(step - 4)
                if 0 <= step - 3 < NKT:
                    st_t2(step - 3)
                if 0 <= step - 2 < NKT:
                    st_m(step - 2)
                if 0 <= step - 1 < NKT:
                    st_t1(step - 1)
                if step < NKT:
                    st_load(step)
            pov = psF[0:16, 2:4, :].rearrange("p b (h d) -> p (b h) d", h=4)
            S.op("dve", [], [("pf", 2), ("pf", 3), "rec"], lambda e: e.reciprocal(out=rec[0:16, :].rearrange("p (h a) -> p h a", a=1), in_=pov[:, :, 64:65]))
            S.op("dve", ["rec"], [("pf", 2), ("pf", 3), "otok"],
                 lambda e: e.tensor_tensor(out=otok[0:16, :].rearrange("p (h d) -> p h d", h=8), in0=pov[:, :, 0:64], in1=bc_ins(rec[0:16, :], 2, 64), op=ALU.mult))
            so, sok = sumsq(otok[0:16, :], ["otok"], 16, 512)
            ro, rok = rstd(so, sok, 512, 16)
            S.op("act", ["otok", rok], [("tokb", tb, 0)], lambda e: e.activation(out=tokb[tb][0:16, 0:512], in_=otok[0:16, :], func=AF.Copy, scale=ro[0:16]))

        wg_i = [0]
        wd_i = [0]
        NWG = 4

        def fm_chunk(wsrc_r, c, T, bank, rhsT, rkeys):
            bufs_, cnt_, kn = pools["wg"]
            wb = cnt_[0] % len(bufs_)
            cnt_[0] += 1
            S.dma("sp", wg[wb][:], wsrc_r[0][c * 128:(c + 1) * 128, :].rearrange("p (k j) -> p k j", k=8), S.wkeys[wsrc_r[1]], [("wg", wb)])

            def f(e):
                last = None
                for k in range(8):
                    last = e.matmul(psF[:, bank, 0:T], lhsT=wg[wb][:, k, :], rhs=rhsT[:, k, 0:T], start=(k == 0), stop=(k == 7))
                return last
            S.op("pe", [("wg", wb)] + rkeys, [("pf", bank)], f)
            return wb

        def fm_chunk_g(wsrc_r, c, T, bank, rhsT, rkeys, pool="wg"):
            bufs_, cnt_, kn = pools[pool]
            wb = cnt_[0] % len(bufs_)
            cnt_[0] += 1
            wt = bufs_[wb]
            S.dma("sp", wt, wsrc_r[0][c * 128:(c + 1) * 128, :].rearrange("p (k j) -> p k j", k=8), S.wkeys[wsrc_r[1]], [(kn, wb)])
            for k0 in (0, 4):
                def f(e, k0=k0):
                    last = None
                    for k in range(k0, k0 + 4):
                        last = e.matmul(psF[:, bank, 0:T], lhsT=wt[:, k, :], rhs=rhsT[:, k, 0:T], start=(k == 0), stop=(k == 7))
                    return last
                S.op("pe", [(kn, wb)] + rkeys, [("pf", bank)], f)
                yield False

        def tm_pass(wsrc, krows, half, lhs_fn, lhs_keys, tiles, banks, pool="wd"):
            nk = len(krows)
            bufs_, cnt_, kn = pools[pool]
            for kk, kr in enumerate(krows):
                b = cnt_[0] % len(bufs_)
                cnt_[0] += 1
                wt = bufs_[b]
                S.dma("sp", wt, wsrc[0][kr * 128:(kr + 1) * 128, half * 512:(half + 1) * 512], S.wkeys[wsrc[1]], [(kn, b)])

                def f(e, wt=wt, kk=kk):
                    last = None
                    for (P, c0), bank in zip(tiles, banks):
                        last = e.matmul(psF[0:P, bank, :], lhsT=lhs_fn(kk, P, c0), rhs=wt, start=(kk == 0), stop=(kk == nk - 1))
                    return last
                S.op("pe", [(kn, b)] + list(lhs_keys), [("pf", bank) for bank in banks[:len(tiles)]], f)
                yield False

        def resid_add(slot, P, bank, half):
            xt = xh[slot]
            S.op("dve", [], [("pf", bank), ("xh", slot)],
                 lambda e: e.tensor_tensor(out=xt[0:P, half * 512:(half + 1) * 512], in0=psF[0:P, bank, :], in1=xt[0:P, half * 512:(half + 1) * 512], op=ALU.add))

        def block_x(tiles, T, blk, is_prompt):
            akeys = [("actT", td["col0"]) for td in tiles]
            HM = 'fao'
            S.hook_on = 'f' in HM
            S.hook_rate = 3.0
            pre = [load_x(td["x_ap"], td["P"]) for td in tiles]
            slots = []
            for j, td in enumerate(tiles):
                slots.append(td["front"](pre[j]))
                if not late_done[0]:
                    late_done[0] = True
                    convert(conv_late)
            S.hook_rate = 2.0
            last30 = is_prompt and blk == 3
            use45 = last30 or not is_prompt
            if use45:
                S.hook_on = False
                S.drain_to_safe()
            if is_prompt and blk > 0:
                S.op("act", ["uTp"], ["uTp"], lambda e: e.copy(out=uTp[:, :, 0:30], in_=uTp[:, :, 512:542]))
            if not is_prompt:
                for sq in range(2):
                    S.dma("sp", utok[0:30, :], cst[sq], [], ["utok"])
                    S.dma("pool", cv_s[sq, 0:14, :], cst[sq, 16:30, :], [], [], is_out=True)
                    S.op("act", ["utok"], [("dwT", 0)], lambda e: e.copy(out=dwT[0:30, 0, :], in_=utok[0:30, :]))
                    transposes(lambda k: dwT[0:30, 0, k * 128:(k + 1) * 128], 4, 30, 128, uTs[:, :, sq, 0:30], ["uTs"], [("dwT", 0)], 1)

            def tokmajor_rows(c, wb, rows0, nrows):
                def ft(e):
                    last = None
                    for k in range(8):
                        last = e.matmul(psF[0:nrows, 2 + c // 4, (c % 4) * 128:(c % 4 + 1) * 128], lhsT=actT[:, k, rows0:rows0 + nrows],
                                        rhs=wg[wb][:, k, :], start=(k == 0), stop=(k == 7))
                    return last
                S.op("pe", [("wg", wb)] + akeys, [("pf", 2 + c // 4)], ft)

            for ch in range(4):
                ab, gb = ch % 2, (4 if use45 else 2) + ch % 2
                wba = fm_chunk((wcb, "wsc_c"), ch, T, ab, actT, akeys)
                if last30:
                    tokmajor_rows(ch, wba, 482, 30)
                if not is_prompt:
                    tokmajor_rows(ch, wba, 0, 32)
                wbg = fm_chunk((wcb, "wsc_c"), 4 + ch, T, gb, actT, akeys)
                if last30:
                    tokmajor_rows(4 + ch, wbg, 482, 30)
                if not is_prompt:
                    tokmajor_rows(4 + ch, wbg, 0, 32)
                gt, gk = (t1, "t1") if ch % 2 == 0 else (otok, "otok")
                S.op("act", [], [("pf", gb), gk], lambda e, gb=gb, gt=gt: e.activation(out=gt[:, 0:T], in_=psF[:, gb, 0:T], func=AF.Tanh, scale=0.5))
                S.op("dve", [gk], [("pf", ab), gk],
                     lambda e, ab=ab, gt=gt: e.scalar_tensor_tensor(out=gt[:, 0:T], in0=gt[:, 0:T], scalar=1.0, in1=psF[:, ab, 0:T], op0=ALU.add, op1=ALU.mult))
                if is_prompt:
                    S.op("act", [gk], ["uTp"], lambda e, ch=ch, gt=gt: e.activation(out=uTp[:, ch, 30:30 + T], in_=gt[:, 0:T], func=AF.Copy, scale=0.5))
                else:
                    S.op("act", [gk], ["uTs"], lambda e, ch=ch, gt=gt: e.activation(out=uTs[:, ch, :, 30:46], in_=gt[:, 0:32].rearrange("p (s t) -> p s t", s=2), func=AF.Copy, scale=0.5))
            nr = 30 if last30 else (32 if not is_prompt else 0)
            if nr:
                S.op("act", [], [("pf", 3), "t1"], lambda e: e.activation(out=t1[0:nr, :], in_=psF[0:nr, 3, :], func=AF.Tanh, scale=0.5))
                S.op("dve", ["t1"], [("pf", 2), "t2"],
                     lambda e: e.scalar_tensor_tensor(out=t2[0:nr, :], in0=t1[0:nr, :], scalar=1.0, in1=psF[0:nr, 2, :], op0=ALU.add, op1=ALU.mult))
                S.op("dve", ["t2"], ["utok"], lambda e: e.tensor_scalar(utok[0:nr, :], t2[0:nr, :], 0.5, None, ALU.mult))
                if last30:
                    S.dma("pool", cv_p, utok[0:30, :], ["utok"], [], is_out=True)
                else:
                    for sq in range(2):
                        S.dma("pool", cv_s[sq, 14:30, :], utok[sq * 16:(sq + 1) * 16, :], ["utok"], [], is_out=True)
            HALF = ((0, 16), (16, 31))
            for ch in range(4):
                wv = sp_t[:, C_WDW + ch * 31:C_WDW + (ch + 1) * 31]
                for hi, (j0, j1) in enumerate(HALF):
                    S.op("dve", ["sp_t", "ident"], [("diag", hi)],
                         lambda e, wv=wv, j0=j0, j1=j1: e.tensor_tensor(out=diag[:, j0:j1, :], in0=bc_ins(ident[:], 1, j1 - j0),
                                                                        in1=bc_ins(wv[:, j0:j1], 2, 128), op=ALU.mult))
                bank = (4 if use45 else 0) + ch % 2
                for hi, (j0, j1) in enumerate(HALF):
                    def fc(e, ch=ch, bank=bank, j0=j0, j1=j1):
                        last = None
                        if is_prompt:
                            for j in range(j0, j1):
                                last = e.matmul(psF[:, bank, 0:T], lhsT=diag[:, j, :], rhs=uTp[:, ch, j:j + T], start=(j == 0), stop=(j == 30))
                        else:
                            for sq in range(2):
                                for j in range(j0, j1):
                                    last = e.matmul(psF[:, bank, sq * 16:(sq + 1) * 16], lhsT=diag[:, j, :], rhs=uTs[:, ch, sq, j:j + 16],
                                                    start=(j == 0 and sq == 0), stop=(j == 30), skip_group_check=True)
                        return last
                    S.op("pe", [("diag", hi), "uTp", "uTs"], [("pf", bank)], fc)
                S.op("act", ["sp_t"], [("pf", bank), ("dwT", ch)],
                     lambda e, ch=ch, bank=bank: e.activation(out=dwT[:, ch, 0:T], in_=psF[:, bank, 0:T], func=AF.Identity, bias=sp_t[:, C_BDW + ch:C_BDW + ch + 1]))
            S.hook_on = 'a' in HM
            est_ops = sum(td["nops"] for td in tiles) + 16
            S.hook_rate = min(3.0, max(0.2, (S.hook_total - S.hook_steps) / float(est_ops)))
            def gen_convln(td, tb):
                P, c0 = td["P"], td["col0"]
                pk = ("pb", 1)

                def ftr(e):
                    last = None
                    for ch in range(4):
                        last = e.transpose(psB[0:P, 1, ch * 128:(ch + 1) * 128], dwT[:, ch, c0:c0 + P], ident[:, :])
                    return last
                S.op("pe", [("dwT", ch) for ch in range(4)] + ["ident"], [pk], ftr)
                s1, s1k = newstat()
                S.op("act", [], [pk, "t1", s1k], lambda e: e.activation(out=t1[0:P, :], in_=psB[0:P, 1, 0:512], func=AF.Copy, accum_out=s1[0:P]))
                yield
                s2, s2k = sumsq(t1[0:P, :], ["t1"], P, 512)
                mean, mk = newstat()
                msq, msk = newstat()
                ve, vek = newstat()
                rs_, rsk = newstat()
                S.op("dve", [s1k], [mk], lambda e: e.tensor_scalar(mean[0:P], s1[0:P], 1.0 / 512, None, ALU.mult))
                S.op("dve", [mk], [msk], lambda e: e.tensor_tensor(out=msq[0:P], in0=mean[0:P], in1=mean[0:P], op=ALU.mult))
                yield
                S.op("dve", [s2k, msk], [vek], lambda e: e.scalar_tensor_tensor(out=ve[0:P], in0=s2[0:P], scalar=1.0 / 512, in1=msq[0:P], op0=ALU.mult, op1=ALU.subtract))
                S.op("dve", [vek], [vek], lambda e: e.tensor_scalar(ve[0:P], ve[0:P], EPS, None, ALU.add))
                S.op("pool", [vek, "mhalf"], [rsk], lambda e: e.tensor_tensor(rs_[0:P], ve[0:P], mhalf[0:P], ALU.pow))
                yield
                S.op("dve", [mk, rsk, "t1"], ["t1"], lambda e: e.tensor_scalar(t1[0:P, :], t1[0:P, :], mean[0:P], rs_[0:P], ALU.subtract, ALU.mult))
                yield
                S.op("dve", ["t1", "bc_t"], ["t1"], lambda e: e.tensor_tensor(out=t1[0:P, :], in0=t1[0:P, :], in1=bc_t[0:P, B_GCN:B_GCN + 512], op=ALU.mult))
                yield
                S.op("dve", ["t1", "bc_t"], ["t1"], lambda e: e.tensor_tensor(out=t1[0:P, :], in0=t1[0:P, :], in1=bc_t[0:P, B_BCN:B_BCN + 512], op=ALU.add))
                yield
                S.op("act", ["t1"], ["t2"], lambda e: e.activation(out=t2[0:P, :], in_=t1[0:P, :], func=AF.Tanh, scale=0.5))
                yield
                S.op("dve", ["t1", "t2"], ["t2"], lambda e: e.scalar_tensor_tensor(out=t2[0:P, :], in0=t2[0:P, :], scalar=1.0, in1=t1[0:P, :], op0=ALU.add, op1=ALU.mult))
                yield
                sc, sck = sumsq(t2[0:P, :], ["t2"], P, 512, scale=0.5)
                rc, rck = rstd(sc, sck, 512, P)
                yield
                S.op("act", ["t2", rck], [("tokb", tb, 1)], lambda e: e.activation(out=tokb[tb][0:P, 512:1024], in_=t2[0:P, :], func=AF.Copy, scale=rc[0:P]))
                yield

            tbs = []
            for td in tiles:
                tbs.append(tokb_i[0] % 2)
                tokb_i[0] += 1
            S.run_side(gen_convln(tiles[0], tbs[0]))
            for j, td in enumerate(tiles):
                P, c0 = td["P"], td["col0"]
                tb = tbs[j]
                S.hook2 = gen_convln(tiles[j + 1], tbs[j + 1]) if j + 1 < len(tiles) else None
                td["attn"](tb)
                if S.hook2 is not None:
                    S.run_side(S.hook2)
                    S.hook2 = None
                transposes(lambda k, tb=tb, P=P: tokb[tb][0:P, k * 128:(k + 1) * 128], 8, P, 128, actT[:, :, c0:c0 + P], [("actT", c0)],
                           [("tokb", tb, 0), ("tokb", tb, 1)], 0, gain=sp_t[:, C_GMIX:C_GMIX + 8])
            tl = [(td["P"], td["col0"]) for td in tiles]
            S.hook_on = 'o' in HM
            for half in range(2):
                for _ in tm_pass((wob, "wsc_o"), list(range(8)), half, lambda k, P, c0: actT[:, k, c0:c0 + P], akeys, tl, [0, 1, 2, 3]):
                    pass
                for t, (P, c0) in enumerate(tl):
                    resid_add(slots[t], P, t, half)
            S.hook_on = False
            return slots

        def gen_ffn(tiles, slots, T, deep=False):
            gp, dp = ("wgS", "wdS") if deep else ("wg", "wd")
            fkeys = [("fT", td["col0"]) for td in tiles]
            tl = [(td["P"], td["col0"]) for td in tiles]
            for t, td in enumerate(tiles):
                P, c0 = td["P"], td["col0"]
                xt = xh[slots[t]]
                sf, sfk = sumsq(xt[0:P, :], [("xh", slots[t])], P, D)
                rf, rfk = rstd(sf, sfk, D, P)
                S.op("act", [("xh", slots[t]), rfk], ["fbuf"],
                     lambda e, P=P, xt=xt, rf=rf: e.activation(out=fbuf[0:P, :], in_=xt[0:P, :], func=AF.Copy, scale=rf[0:P]))
                transposes(lambda k, P=P: fbuf[0:P, k * 128:(k + 1) * 128], 8, P, 128, fT[:, :, c0:c0 + P], [("fT", c0)],
                           ["fbuf"], 1, gain=sp_t[:, C_LNFFN:C_LNFFN + 8])
                yield True
            pairs = [list(range(i, min(i + 2, len(tiles)))) for i in range(0, len(tiles), 2)]
            for r in range(2):
                for cc in range(11):
                    c = r * 11 + cc
                    yield from fm_chunk_g((wgb, "wsc_g"), c, T, 4, fT, fkeys, gp)
                    yield from fm_chunk_g((wub, "wsc_u"), c, T, 5, fT, fkeys, gp)
                    tmp, tk = (t4[:, 0:T], "t4") if cc % 2 == 0 else (fbuf[:].bitcast(F32)[:, 0:T], "fbuf")
                    S.op("act", [], [("pf", 4), tk], lambda e, tmp=tmp: e.activation(out=tmp, in_=psF[:, 4, 0:T], func=AF.Tanh, scale=0.5))
                    S.op("dve", [tk], [("pf", 4), tk],
                         lambda e, tmp=tmp: e.scalar_tensor_tensor(out=tmp, in0=tmp, scalar=1.0, in1=psF[:, 4, 0:T], op0=ALU.add, op1=ALU.mult))
                    S.op("dve", [tk], [("pf", 5), ("aT", cc)],
                         lambda e, cc=cc, tmp=tmp: e.scalar_tensor_tensor(out=aT[:, cc, 0:T], in0=tmp, scalar=0.5, in1=psF[:, 5, 0:T], op0=ALU.mult, op1=ALU.mult))
                    yield True
                for pr in pairs:
                    ptl = [tl[t] for t in pr]
                    for half in range(2):
                        yield from tm_pass((wdb, "wsc_d"), [r * 11 + k for k in range(11)], half, lambda k, P, c0: aT[:, k, c0:c0 + P],
                                           [("aT", k) for k in range(11)], ptl, [4, 5], dp)
                        for j, t in enumerate(pr):
                            resid_add(slots[t], ptl[j][0], 4 + j, half)
                        yield True
            for t, td in enumerate(tiles):
                P = td["P"]
                xt = xh[slots[t]]
                sy, syk = sumsq(xt[0:P, :], [("xh", slots[t])], P, D)
                ry, ryk = rstd(sy, syk, D, P)
                S.op("dve", [ryk, "bc_t"], [("xh", slots[t])],
                     lambda e, P=P, xt=xt, ry=ry: e.scalar_tensor_tensor(out=xt[0:P, :], in0=xt[0:P, :], scalar=ry[0:P], in1=bc_t[0:P, B_GFIN:B_GFIN + D], op0=ALU.mult, op1=ALU.mult))
                S.dma("pool", td["y_dst"], xt[0:P, :], [("xh", slots[t])], [], is_out=True)
                yield True

        QTs = [sb(f"QT{i}", [128, 8, 128], BF16) for i in range(4)]
        print("SBUF bytes remaining", nc.sbuf_bytes_remaining)

        blocks = []
        nblk = 4 if not dbg else (dbg if dbg > 0 else 0)
        for blk in range(nblk):
            tiles = []
            for j in range(4):
                i = blk * 4 + j
                td = dict(P=128, col0=j * 128, y_dst=y_p[i * 128:(i + 1) * 128, :])
                td["x_ap"] = xp[i * 128:(i + 1) * 128, :]
                td["front"] = (lambda slot, i=i, j=j: front(slot, 128, j * 128, i,
                                                      kv_p[i * 128:(i + 1) * 128, :], kr_p[i * 128:(i + 1) * 128, :],
                                                      KT[0:96, :, i * 128:(i + 1) * 128], VA[:, i, :, 0:64], [("KT", i)], [("VA", i)],
                                                      QTs[j], ("QT", j)))
                td["attn"] = (lambda tb, i=i, j=j: attn_prompt(i, tb, QTs[j], ("QT", j)))
                td["nops"] = 2 + 16 * ((i + 4) // 4)
                tiles.append(td)
            blocks.append((tiles, 512, blk, True))
        if do_sample:
            tiles = []
            for sq in range(2):
                td = dict(P=16, col0=sq * 16, y_dst=y_s[sq * 16:(sq + 1) * 16, :])
                td["x_ap"] = xs[sq * 16:(sq + 1) * 16, :]
                td["front"] = (lambda slot, sq=sq: front(slot, 16, sq * 16, 16,
                                                   kv_s[sq * 16:(sq + 1) * 16, :], kr_s[sq * 16:(sq + 1) * 16, :],
                                                   KTn[sq][0:96, :, 0:16], VAn[sq][0:16, :, 0:64], [("KTn", sq)], [("VAn", sq)],
                                                   QTs[sq], ("QT", sq)))
                td["attn"] = (lambda tb, sq=sq: attn_sample(sq, tb, QTs[sq], ("QT", sq)))
                td["nops"] = 2 + 33 * 6
                tiles.append(td)
            blocks.append((tiles, 32, 4, False))

        samp_keys = [(n, i) for n in ("KTs", "VAs", "cbuf", "krc") for i in range(4)] + \
                    [(n, i) for n in ("KTn", "VAn", "ckvT2", "Ktok2") for i in range(2)] + ["uTs"]
        pending = None
        pending_units = 0
        for (tiles, T, blk, is_prompt) in blocks:
            if not is_prompt:
                S.op("pool", [], samp_keys + [("KT", i) for i in range(NTILE)], lambda e: e.memset(scr[:], 0.0))
                for i in range(4):
                    S.op("pool", [], [("VAs", i)], lambda e, i=i: e.memset(VAs[i], 1.0))
                for i in range(2):
                    S.op("pool", [], [("VAn", i)], lambda e, i=i: e.memset(VAn[i], 1.0))
            S.hook = pending
            S.hook_steps = 0
            S.hook_acc = 0.0
            S.hook_total = pending_units
            slots = block_x(tiles, T, blk, is_prompt)
            S.hook = None
            if pending is not None:
                for _ in pending:
                    pass
            last_blk = (tiles is blocks[-1][0]) and not is_prompt
            if last_blk:
                S.op("pool", [], samp_keys + [("wgS", i) for i in range(len(wgS))] + [("wdS", i) for i in range(len(wdS))],
                     lambda e: e.memset(scr[:], 0.0))
            pending = gen_ffn(tiles, slots, T, deep=last_blk)
            npairs = (len(tiles) + 1) // 2
            pending_units = 2 * len(tiles) + NFF * 5 + 2 * npairs * 2 * 12
        if pending is not None:
            for _ in pending:
                pass
        S.finish()
    return nc


def _consts():
    ident = np.eye(128, dtype=np.float32).astype(ml_dtypes.bfloat16)
    inv = (1.0 / (np.float32(10000.0) ** (np.arange(0, 32, 2, dtype=np.float32) / np.float32(32)))).astype(np.float32)
    pos = np.zeros((128, 17), np.float32)
    for i in range(16):
        pos[:, i] = i * 128 + np.arange(128)
    pos[:, 16] = PAST + (np.arange(128) % 16)
    return ident, inv, pos


def _tile_cols(w):
    C = w.shape[1]
    return np.ascontiguousarray(w.reshape(8, 128, C // 128, 128).transpose(2, 1, 0, 3))


def kernel(x_prompt, x_sample, cache_kv_latent, cache_k_rope, state_conv,
           ln_mix, w_in, g_q, w_uq, g_kv, w_uk, w_uv, w_dw, b_dw, g_cn, b_cn, g_om, g_oc,
           w_out, ln_ffn, w_gate, w_up, w_down, g_final, _dbg=False):
    f = lambda a: np.ascontiguousarray(np.asarray(a, dtype=np.float32))
    ident, inv, pos = _consts()
    smallp = np.zeros((128, C_END), np.float32)
    smallp[:, C_LNMIX:C_LNMIX + 8] = f(ln_mix)[0].reshape(8, 128).T
    smallp[:, C_GQ:C_GQ + 3] = f(g_q)[0].reshape(3, 128).T
    smallp[:, C_GMIX:C_GMIX + 4] = f(g_om)[0].reshape(4, 128).T
    smallp[:, C_GMIX + 4:C_GMIX + 8] = f(g_oc)[0].reshape(4, 128).T
    smallp[:, C_LNFFN:C_LNFFN + 8] = f(ln_ffn)[0].reshape(8, 128).T
    smallp[:, C_BDW:C_BDW + 4] = f(b_dw)[0].reshape(4, 128).T
    smallp[:, C_WDW:C_WDW + 124] = f(w_dw)[0].reshape(31, 4, 128).transpose(2, 1, 0).reshape(128, 124)
    bcp = np.zeros((1, B_END), np.float32)
    bcp[0, B_GKV:B_GKV + 256] = f(g_kv)[0]
    bcp[0, B_GCN:B_GCN + 512] = f(g_cn)[0]
    bcp[0, B_BCN:B_BCN + 512] = f(b_cn)[0]
    bcp[0, B_GFIN:B_GFIN + 1024] = f(g_final)
    bcp[0, B_INV:B_INV + 16] = inv
    w_in0 = f(w_in)[0]
    shared = dict(
        w_in=np.ascontiguousarray(w_in0[:, 0:672]), wc_r=_tile_cols(w_in0[:, 672:1696]),
        w_uq=f(w_uq)[0], w_uk=f(w_uk)[0].reshape(256, 512), w_uv=f(w_uv)[0].reshape(256, 512),
        w_out=f(w_out)[0], wg_r=_tile_cols(f(w_gate)[0]), wu_r=_tile_cols(f(w_up)[0]), w_down=f(w_down)[0],
        smallp=smallp, bcp=bcp, postab=pos, identd=ident)
    xp_, xs_ = f(x_prompt), f(x_sample)
    ckv_, ckr_, cst_ = f(cache_kv_latent)[0], f(cache_k_rope)[0], f(state_conv)[0]
    in_maps = []
    for c in range(N_CORES):
        m = dict(shared)
        m["xp"] = xp_[c]
        m["xs"] = xs_[2 * c:2 * c + 2].reshape(32, D)
        m["ckv_c"] = ckv_[2 * c:2 * c + 2]
        m["ckr_c"] = ckr_[2 * c:2 * c + 2]
        m["cst"] = cst_[2 * c:2 * c + 2]
        in_maps.append(m)
    nc = build_program(dbg=_dbg)
    res = run_bass_kernel_spmd(nc, in_maps, core_ids=list(range(N_CORES)))
    R = res.results
    y_prompt = np.stack([R[c]["y_p"] for c in range(N_CORES)])
    y_sample = np.concatenate([R[c]["y_s"].reshape(2, 16, D) for c in range(N_CORES)])
    kvp = np.stack([R[c]["kv_p"] for c in range(N_CORES)])[None]
    krp = np.stack([R[c]["kr_p"] for c in range(N_CORES)])[None]
    cvp = np.stack([R[c]["cv_p"] for c in range(N_CORES)])[None]
    kvs = np.concatenate([R[c]["kv_s"].reshape(2, 16, 256) for c in range(N_CORES)])[None]
    krs = np.concatenate([R[c]["kr_s"].reshape(2, 16, 32) for c in range(N_CORES)])[None]
    cvs = np.concatenate([R[c]["cv_s"] for c in range(N_CORES)])[None]
    return tuple(np.ascontiguousarray(a.astype(np.float32)) for a in (y_prompt, y_sample, kvp, krp, cvp, kvs, krs, cvs))
```

```python
import math
from contextlib import ExitStack

import numpy as np
import ml_dtypes

import concourse.bass as bass
import concourse.mybir as mybir
from concourse.bass_utils import run_bass_kernel_spmd

F32, BF16, I32 = mybir.dt.float32, mybir.dt.bfloat16, mybir.dt.int32
AF, ALU = mybir.ActivationFunctionType, mybir.AluOpType

EPS = 1e-6
D = 1024
SEQ = 2048
NTILE = 16
DFF = 2816
NFF = 22
PAST = 4096
SCALE = 1.0 / math.sqrt(96.0)
N_CORES = 8
NDMA = 48
NDMA_SP = 28

C_LNMIX, C_GQ, C_GMIX, C_LNFFN, C_BDW, C_WDW, C_END = 0, 8, 11, 19, 27, 31, 155
B_GKV, B_GCN, B_BCN, B_GFIN, B_INV, B_END = 0, 256, 768, 1280, 2304, 2320


class Op:
    __slots__ = ("key", "val", "clk")

    def __init__(self, key, val, clk):
        self.key, self.val, self.clk = key, val, clk


class Sched:
    def __init__(self, nc, es):
        self.nc = nc
        self.E = {"pe": nc.tensor, "act": nc.scalar, "dve": nc.vector, "pool": nc.gpsimd, "sp": nc.sync}
        self.sem = {k: es.enter_context(nc.semaphore("sem_" + k)) for k in self.E}
        self.cnt = {k: 0 for k in self.E}
        self.clk = {k: {} for k in self.E}
        self.lastw, self.readers = {}, {}
        self.dsem = [es.enter_context(nc.semaphore(f"dsem{i}")) for i in range(NDMA)]
        self.dcnt = [0] * NDMA
        self.dlast = [None] * NDMA
        self.dnext = 0
        self.dnext_sw = 0
        self.out_ops = []
        self.hook = None
        self.hook_safe = True
        self.hook_rate = 1.0
        self.hook2 = None
        self._in_hook2 = False
        self.hook_acc = 0.0
        self.hook_steps = 0
        self.wkeys = {}
        self.hook_on = False
        self._in_hook = False

    def _semh(self, key):
        return self.sem[key] if isinstance(key, str) else self.dsem[key[1]]

    def _wait(self, eng, op):
        if op is None:
            return
        c = self.clk[eng]
        if c.get(op.key, 0) >= op.val:
            return
        if op.key == "pe" and eng == "pe":
            return
        self.E[eng].wait_ge(self._semh(op.key), op.val)
        for k, v in op.clk.items():
            if c.get(k, 0) < v:
                c[k] = v
        c[op.key] = op.val

    def _deps(self, reads, writes):
        deps = []
        for r in reads:
            w = self.lastw.get(r)
            if w is not None:
                deps.append(w)
        for k in writes:
            w = self.lastw.get(k)
            if w is not None:
                deps.append(w)
            deps.extend(self.readers.get(k, ()))
        return deps

    def _commit(self, op, reads, writes):
        for r in reads:
            self.readers.setdefault(r, []).append(op)
        for k in writes:
            self.lastw[k] = op
            self.readers[k] = []

    def op(self, eng, reads, writes, fn):
        for d in self._deps(reads, writes):
            self._wait(eng, d)
        inst = fn(self.E[eng])
        self.cnt[eng] += 1
        inst.then_inc(self.sem[eng], 1)
        o = Op(eng, self.cnt[eng], dict(self.clk[eng]))
        self._commit(o, reads, writes)
        if eng == "pe" and self.hook is not None and self.hook_on and not self._in_hook and not self._in_hook2:
            self.hook_acc += self.hook_rate
            while self.hook_acc >= 1.0:
                self.hook_acc -= 1.0
                self.step_hook()
        if eng == "pe" and self.hook2 is not None and not self._in_hook2 and not self._in_hook:
            self._in_hook2 = True
            try:
                next(self.hook2, None)
            finally:
                self._in_hook2 = False
        return o

    def step_hook(self):
        self._in_hook = True
        try:
            v = next(self.hook, None)
            self.hook_steps += 1
            self.hook_safe = True if v is None else bool(v)
        finally:
            self._in_hook = False

    def run_side(self, gen):
        self._in_hook2 = True
        try:
            for _ in gen:
                pass
        finally:
            self._in_hook2 = False

    def drain_to_safe(self):
        while self.hook is not None and not self.hook_safe:
            self.step_hook()

    def dma(self, q, out, in_, reads, writes, is_out=False, slow=False):
        if q == "pool":
            i = NDMA_SP + self.dnext_sw
            self.dnext_sw = (self.dnext_sw + 1) % (NDMA - NDMA_SP)
        else:
            i = self.dnext
            self.dnext = (i + 1) % NDMA_SP
        self._wait(q, self.dlast[i])
        for d in self._deps(reads, writes):
            self._wait(q, d)
        self.dcnt[i] += 16
        if slow:
            self.E[q].dma_start(out=out, in_=in_, allow_slow_non_contiguous=True).then_inc(self.dsem[i], 16)
        else:
            self.E[q].dma_start(out=out, in_=in_).then_inc(self.dsem[i], 16)
        o = Op(("d", i), self.dcnt[i], dict(self.clk[q]))
        self.dlast[i] = o
        self._commit(o, reads, writes)
        if is_out:
            self.out_ops.append(o)
        return o

    def finish(self):
        for o in self.out_ops:
            self._wait("sp", o)
        for e in ("pe", "act", "dve", "pool"):
            if self.cnt[e] > 0:
                self._wait("sp", Op(e, self.cnt[e], {}))


def bc_ins(a, pos, n):
    apl = [list(x) for x in a.ap]
    apl.insert(pos, [0, n])
    return bass.AP(a.tensor, a.offset, apl)


def build_program(do_sample=True, dbg=False):
    nc = bass.Bass("TRN2", target_bir_lowering=False)

    def din(name, shape, dt=F32):
        return nc.dram_tensor(name, list(shape), dt, kind="ExternalInput").ap()

    def dout(name, shape):
        return nc.dram_tensor(name, list(shape), F32, kind="ExternalOutput").ap()

    xp = din("xp", [SEQ, D])
    xs = din("xs", [32, D])
    ckv_c = din("ckv_c", [2, PAST, 256])
    ckr_c = din("ckr_c", [2, PAST, 32])
    cst = din("cst", [2, 30, 512])
    w_in = din("w_in", [D, 672])
    wc_r = din("wc_r", [8, 128, 8, 128])
    w_uq = din("w_uq", [384, 768])
    w_uk = din("w_uk", [256, 512])
    w_uv = din("w_uv", [256, 512])
    w_out = din("w_out", [D, D])
    wg_r = din("wg_r", [NFF, 128, 8, 128])
    wu_r = din("wu_r", [NFF, 128, 8, 128])
    w_down = din("w_down", [DFF, D])
    smallp = din("smallp", [128, C_END])
    bcp = din("bcp", [1, B_END])
    postab = din("postab", [128, 17])
    identd = din("identd", [128, 128], BF16)

    def dscr(name, shape):
        return nc.dram_tensor(name, list(shape), BF16).ap()

    wgb = dscr("wgb", [NFF * 128, 1024])
    wub = dscr("wub", [NFF * 128, 1024])
    wcb = dscr("wcb", [8 * 128, 1024])
    wob = dscr("wob", [D, D])
    wdb = dscr("wdb", [DFF, D])

    y_p = dout("y_p", [SEQ, D])
    y_s = dout("y_s", [32, D])
    kv_p = dout("kv_p", [SEQ, 256])
    kr_p = dout("kr_p", [SEQ, 32])
    cv_p = dout("cv_p", [30, 512])
    kv_s = dout("kv_s", [32, 256])
    kr_s = dout("kr_s", [32, 32])
    cv_s = dout("cv_s", [2, 30, 512])

    with ExitStack() as es:
        S = Sched(nc, es)

        def sb(name, shape, dt=F32):
            return es.enter_context(nc.sbuf_tensor(name, list(shape), dt))

        psF = es.enter_context(nc.psum_tensor("psF", [128, 6, 512], F32))
        psB = es.enter_context(nc.psum_tensor("psB", [128, 2, 1024], BF16))

        ident = sb("ident", [128, 128], BF16)
        sp_t = sb("sp_t", [128, C_END])
        bc_t = sb("bc_t", [128, B_END])
        pos_t = sb("pos_t", [128, 17])
        cos_t = sb("cos_t", [128, 17, 16])
        sin_t = sb("sin_t", [128, 17, 16])
        mhalf = sb("mhalf", [128, 1])
        scr = sb("scr", [128, 1])
        stat = sb("stat", [128, 512])
        w_in_a = sb("w_in_a", [128, 8, 672], BF16)
        w_uq_t = sb("w_uq_t", [128, 3, 768], BF16)
        w_uk_t = sb("w_uk_t", [128, 2, 512], BF16)
        w_uv_t = sb("w_uv_t", [128, 2, 512], BF16)
        KT = sb("KT", [128, 8, SEQ], BF16)
        VA = sb("VA", [128, NTILE, 8, 65], BF16)
        diag = sb("diag", [128, 31, 128], BF16)
        uTp = sb("uTp", [128, 4, 542], BF16)
        NXH = 8
        xh = [sb(f"xh{i}", [128, D]) for i in range(NXH)]
        tokb = [sb(f"tokb{i}", [128, D], BF16) for i in range(2)]
        actT = sb("actT", [128, 8, 512], BF16)
        aT = sb("aT", [128, 11, 512], BF16)
        fT = sb("fT", [128, 8, 512], BF16)
        fbuf = sb("fbuf", [128, D], BF16)
        t4 = sb("t4", [128, 512])
        wg = [sb(f"wg{i}", [128, 8, 128], BF16) for i in range(4)]
        wd = [sb(f"wd{i}", [128, 512], BF16) for i in range(5)]
        cqn = sb("cqn", [128, 384], BF16)
        cqT = sb("cqT", [128, 3, 128], BF16)
        Qtok = sb("Qtok", [128, 8, 96], BF16)
        ropec = sb("ropec", [128, 8, 2, 16])
        ropes = sb("ropes", [128, 8, 2, 16])
        ckvf = [sb(f"ckvf{i}", [128, 256]) for i in range(2)]
        ckvb = sb("ckvb", [128, 256], BF16)
        ckvT = sb("ckvT", [128, 2, 128], BF16)
        krf = [sb(f"krf{i}", [128, 32]) for i in range(2)]
        Ktok = sb("Ktok", [128, 8, 96], BF16)
        PT = [sb(f"PT{i}", [128, 4, 128], BF16) for i in range(2)]
        otok = sb("otok", [128, 512])
        rec = sb("rec", [128, 8])
        dwT = sb("dwT", [128, 4, 512], BF16)
        t1 = sb("t1", [128, 512])
        t2 = sb("t2", [128, 512])
        junk = sb("junk", [128, D], BF16)
        utok = sb("utok", [32, 512])
        KTs = [KT[:, b // 2, (b % 2) * 1024:(b % 2 + 1) * 1024].rearrange("p (h t) -> p h t", h=8) for b in range(4)]
        VAs = [KT[:, 2, b * 520:(b + 1) * 520].rearrange("p (h d) -> p h d", h=8) for b in range(3)] + \
              [KT[:, 3, 0:520].rearrange("p (h d) -> p h d", h=8)]
        VAn = [KT[:, 3, (1 + b) * 520:(2 + b) * 520].rearrange("p (h d) -> p h d", h=8) for b in range(2)]
        cbuf = [KT[:, 4, b * 256:(b + 1) * 256] for b in range(4)]
        ckvT2 = [KT[:, 4, 1024 + b * 256:1024 + (b + 1) * 256].rearrange("p (k t) -> p k t", k=2) for b in range(2)]
        krc = [KT[:, 5, b * 64:(b + 1) * 64].bitcast(F32) for b in range(4)]
        KTn = [KT[:, 6, b * 128:(b + 1) * 128].rearrange("p (h t) -> p h t", h=8) for b in range(2)]
        Ktok2 = [KT[:, 6, 256 + b * 768:256 + (b + 1) * 768].rearrange("p (h d) -> p h d", h=8) for b in range(2)]
        uTs = KT[:, 5, 1024:1024 + 368].rearrange("p (c s t) -> p c s t", c=4, s=2)

        wgS = [KT[:, r, c * 1024:(c + 1) * 1024].rearrange("p (k j) -> p k j", k=8) for r in range(6) for c in range(2)]
        wdS = [KT[:, 6 + b // 4, (b % 4) * 512:(b % 4 + 1) * 512] for b in range(8)]
        pools = {"wg": ([w_[:] for w_ in wg], [0], "wg"), "wgS": (wgS, [0], "wgS"), "wd": ([w_[:] for w_ in wd], [0], "wd"), "wdS": (wdS, [0], "wdS")}

        stat_i = [0]

        def newstat():
            i = stat_i[0] % 512
            stat_i[0] += 1
            return stat[:, i:i + 1], ("st", i)

        def rstd(ssq, ssq_k, n, P=128):
            t, tk = newstat()
            r, rk = newstat()
            S.op("dve", [ssq_k], [tk], lambda e: e.tensor_scalar(t[0:P], ssq[0:P], 1.0 / n, EPS, ALU.mult, ALU.add))
            S.op("pool", [tk, "mhalf"], [rk], lambda e: e.tensor_tensor(r[0:P], t[0:P], mhalf[0:P], ALU.pow))
            return r, rk

        def sumsq(src, src_keys, P, n, scale=1.0, excl=()):
            s, sk = newstat()
            S.op("act", list(src_keys), [sk, "junk"] + list(excl),
                 lambda e: e.activation(out=junk[0:P, 0:n], in_=src, func=AF.Square, scale=scale, accum_out=s[0:P]))
            return s, sk

        def transposes(src_fn, nk, P, width, dst, dst_keys, src_keys, bank, gain=None):
            pk = ("pb", bank)

            def f(e):
                last = None
                for k in range(nk):
                    last = e.transpose(psB[0:width, bank, k * P:(k + 1) * P], src_fn(k), ident[0:P, 0:P])
                return last
            S.op("pe", list(src_keys) + ["ident"], [pk], f)
            src = psB[0:width, bank, 0:nk * P].rearrange("p (k t) -> p k t", k=nk)
            if gain is None:
                S.op("act", [], [pk] + list(dst_keys), lambda e: e.copy(out=dst, in_=src))
            else:
                g = bc_ins(gain, 2, P)
                S.op("dve", ["sp_t"], [pk] + list(dst_keys), lambda e: e.tensor_tensor(out=dst, in0=src, in1=g, op=ALU.mult))

        S.dma("sp", ident[:], identd, [], ["ident"])
        S.dma("sp", sp_t[:], smallp, [], ["sp_t"])
        S.dma("sp", pos_t[:], postab, [], ["pos_t"])
        S.dma("sp", bc_t[:], bass.AP(bcp.tensor, 0, [[0, 128], [1, B_END]]), [], ["bc_t"])
        S.dma("pool", w_in_a[:], w_in.rearrange("(k p) c -> p k c", p=128), [], ["w_in_a"])
        S.dma("pool", w_uq_t[:], w_uq.rearrange("(k p) c -> p k c", p=128), [], ["w_uq"])
        S.dma("pool", w_uk_t[:], w_uk.rearrange("(k p) c -> p k c", p=128), [], ["w_uk"])
        S.dma("pool", w_uv_t[:], w_uv.rearrange("(k p) c -> p k c", p=128), [], ["w_uv"])
        def convert(items):
            for dst, src, key, rows in items:
                step = 512 if rows == 1024 else 704
                for r0 in range(0, rows, step):
                    S.dma("pool", dst[r0:r0 + step, :], src[r0:r0 + step, :], [], [(key, r0)])
                S.wkeys[key] = [(key, r0) for r0 in range(0, rows, step)]

        conv_late = [(wob, w_out, "wsc_o", 1024),
                     (wgb, wg_r.rearrange("c p k j -> (c p) (k j)"), "wsc_g", NFF * 128),
                     (wub, wu_r.rearrange("c p k j -> (c p) (k j)"), "wsc_u", NFF * 128),
                     (wdb, w_down, "wsc_d", DFF)]
        late_done = [False]
        S.op("dve", [], ["mhalf"], lambda e: e.memset(mhalf[:], -0.5))
        S.op("pool", [], [("VA", i) for i in range(NTILE)], lambda e: e.memset(VA[:, :, :, 64:65], 1.0))
        convert([(wcb, wc_r.rearrange("c p k j -> (c p) (k j)"), "wsc_c", 1024)])
        S.op("pool", [], ["uTp"], lambda e: e.memset(uTp[:], 0.0))
        S.op("dve", ["sp_t"], ["sp_t"],
             lambda e: e.tensor_scalar(sp_t[:, C_GMIX + 4:C_GMIX + 8], sp_t[:, C_GMIX + 4:C_GMIX + 8], 0.5, None, ALU.mult))

        TWO_PI = 2.0 * math.pi
        C1 = 6.28125
        C2 = TWO_PI - C1
        ang = t1[:, 0:272].rearrange("p (a b) -> p a b", a=17)
        wk = t2[:, 0:272].rearrange("p (a b) -> p a b", a=17)
        wk2 = t4[:, 0:272].rearrange("p (a b) -> p a b", a=17)
        ki = xh[0][:, 0:272].bitcast(I32).rearrange("p (a b) -> p a b", a=17)
        inv_b = bc_ins(bc_t[:, B_INV:B_INV + 16], 1, 17)
        pos_b = bc_ins(pos_t[:, 0:17], 2, 16)
        S.op("dve", ["bc_t", "pos_t"], ["t1"], lambda e: e.tensor_tensor(out=ang, in0=pos_b, in1=inv_b, op=ALU.mult))
        for tab, shift in ((sin_t, 0.0), (cos_t, math.pi / 2)):
            S.op("dve", ["t1"], ["t2"], lambda e: e.tensor_scalar(wk, ang, shift, 1.0 / TWO_PI, ALU.add, ALU.mult))
            S.op("dve", ["t2"], [("xh", 0)], lambda e: e.tensor_copy(out=ki, in_=wk))
            S.op("dve", [("xh", 0)], ["t2"], lambda e: e.tensor_copy(out=wk, in_=ki))
            S.op("dve", ["t2", "t1"], ["t4"], lambda e: e.scalar_tensor_tensor(out=wk2, in0=wk, scalar=-C1, in1=ang, op0=ALU.mult, op1=ALU.add))
            S.op("dve", ["t4", "t2"], ["t4"], lambda e: e.scalar_tensor_tensor(out=wk2, in0=wk, scalar=-C2, in1=wk2, op0=ALU.mult, op1=ALU.add))
            if shift != 0.0:
                S.op("dve", ["t4"], ["t4"], lambda e: e.tensor_scalar(wk2, wk2, shift, None, ALU.add))
            S.op("dve", ["t4"], ["t2"], lambda e: e.tensor_scalar(wk, wk2, math.pi, -TWO_PI, ALU.is_gt, ALU.mult))
            S.op("dve", ["t2", "t4"], ["t4"], lambda e: e.tensor_tensor(out=wk2, in0=wk2, in1=wk, op=ALU.add))
            S.op("dve", ["t4"], ["t2"], lambda e: e.tensor_scalar(wk, wk2, -math.pi, TWO_PI, ALU.is_lt, ALU.mult))
            S.op("dve", ["t2", "t4"], ["t4"], lambda e: e.tensor_tensor(out=wk2, in0=wk2, in1=wk, op=ALU.add))
            S.op("act", ["t4"], [("tab", id(tab))], lambda e, tab=tab: e.activation(out=tab[:], in_=wk2, func=AF.Sin))
        TABK = [("tab", id(sin_t)), ("tab", id(cos_t))]

        def rope(src, P, ti, out_lo, out_hi, nh, keys_r, keys_w):
            cb = bc_ins(bc_ins(cos_t[0:P, ti, :], 1, 2), 1, nh)
            sbb = bc_ins(bc_ins(sin_t[0:P, ti, :], 1, 2), 1, nh)
            rc = ropec[0:P, 0:nh]
            rs = ropes[0:P, 0:nh]
            S.op("dve", TABK, list(keys_r) + ["ropec"], lambda e: e.tensor_tensor(out=rc, in0=src, in1=cb, op=ALU.mult))
            S.op("dve", TABK, list(keys_r) + ["ropes"], lambda e: e.tensor_tensor(out=rs, in0=src, in1=sbb, op=ALU.mult))
            S.op("dve", ["ropec", "ropes"], list(keys_w),
                 lambda e: e.tensor_tensor(out=out_lo, in0=ropec[0:P, 0:nh, 0, :], in1=ropes[0:P, 0:nh, 1, :], op=ALU.subtract))
            S.op("dve", ["ropec", "ropes"], list(keys_w),
                 lambda e: e.tensor_tensor(out=out_hi, in0=ropec[0:P, 0:nh, 1, :], in1=ropes[0:P, 0:nh, 0, :], op=ALU.add))

        xh_i = [0]

        def load_x(x_ap, P):
            s_ = xh_i[0] % NXH
            xh_i[0] += 1
            S.dma("sp", xh[s_][0:P, :], x_ap, [], [("xh", s_)])
            return s_
        tokb_i = [0]

        def front(x_src, P, col0, ti, kv_dst, kr_dst, kt_dst, va_dst, kt_keys, va_keys, qt, qtk):
            s = x_src
            xk = ("xh", s)
            xt = xh[s]
            ssq, ssqk = sumsq(xt[0:P, :], [xk], P, D)
            r, rk = rstd(ssq, ssqk, D, P)
            tb = tokb_i[0] % 2
            tokb_i[0] += 1
            hn = tokb[tb]
            S.op("act", [xk, rk], [("tokb", tb, 0), ("tokb", tb, 1)], lambda e: e.activation(out=hn[0:P, :], in_=xt[0:P, :], func=AF.Copy, scale=r[0:P]))
            transposes(lambda k: hn[0:P, k * 128:(k + 1) * 128], 8, P, 128, actT[:, :, col0:col0 + P], [("actT", col0)],
                       [("tokb", tb, 0), ("tokb", tb, 1)], 0, gain=sp_t[:, C_LNMIX:C_LNMIX + 8])
            def fa(e):
                last = None
                for k in range(8):
                    e.matmul(psF[0:P, 0, 0:384], lhsT=actT[:, k, col0:col0 + P], rhs=w_in_a[:, k, 0:384], start=(k == 0), stop=(k == 7))
                for k in range(8):
                    last = e.matmul(psF[0:P, 1, 0:288], lhsT=actT[:, k, col0:col0 + P], rhs=w_in_a[:, k, 384:672], start=(k == 0), stop=(k == 7))
                return last
            S.op("pe", [("actT", col0), "w_in_a"], [("pf", 0), ("pf", 1)], fa)
            sq, sqk = sumsq(psF[0:P, 0, 0:384], [], P, 384, excl=[("pf", 0)])
            rq, rqk = rstd(sq, sqk, 384, P)
            S.op("act", [rqk], [("pf", 0), "cqn"], lambda e: e.activation(out=cqn[0:P, :], in_=psF[0:P, 0, 0:384], func=AF.Copy, scale=rq[0:P]))
            transposes(lambda k: cqn[0:P, k * 128:(k + 1) * 128], 3, P, 128, cqT[:, :, 0:P], ["cqT"], ["cqn"], 0,
                       gain=sp_t[:, C_GQ:C_GQ + 3])
            sk_, skk = sumsq(psF[0:P, 1, 0:256], [], P, 256, excl=[("pf", 1)])
            rkv, rkvk = rstd(sk_, skk, 256, P)
            cb_i = ti % 2
            cf = ckvf[cb_i]
            S.op("dve", [rkvk, "bc_t"], [("pf", 1), ("ckvf", cb_i)],
                 lambda e: e.scalar_tensor_tensor(out=cf[0:P, :], in0=psF[0:P, 1, 0:256], scalar=rkv[0:P], in1=bc_t[0:P, B_GKV:B_GKV + 256], op0=ALU.mult, op1=ALU.mult))
            S.op("act", [("ckvf", cb_i)], ["ckvb"], lambda e: e.copy(out=ckvb[0:P, :], in_=cf[0:P, :]))
            kf = krf[cb_i]
            ksrc = psF[0:P, 1, 256:288].rearrange("p (a h d) -> p a h d", a=1, h=2)
            rope(ksrc, P, ti, kf[0:P, 0:16].rearrange("p (a d) -> p a d", a=1), kf[0:P, 16:32].rearrange("p (a d) -> p a d", a=1), 1,
                 [("pf", 1)], [("krf", cb_i)])
            def fq(e):
                last = None
                for hb in range(2):
                    for h4 in range(4):
                        hh = hb * 4 + h4
                        for k in range(3):
                            last = e.matmul(psF[0:P, hb, h4 * 128:h4 * 128 + 96], lhsT=cqT[:, k, 0:P], rhs=w_uq_t[:, k, hh * 96:(hh + 1) * 96],
                                            start=(k == 0), stop=(k == 2))
                return last
            S.op("pe", ["cqT", "w_uq"], [("pf", 0), ("pf", 1)], fq)
            qv = psF[0:P, 0:2, :].rearrange("p b (h d) -> p (b h) d", h=4)
            S.op("act", [], [("pf", 0), ("pf", 1), "Qtok"], lambda e: e.copy(out=Qtok[0:P, :, 0:64], in_=qv[:, :, 0:64]))
            qpe = qv[:, :, 64:96].rearrange("p h (a d) -> p h a d", a=2)
            rope(qpe, P, ti, Qtok[0:P, :, 64:80], Qtok[0:P, :, 80:96], 8, [("pf", 0), ("pf", 1)], ["Qtok"])
            transposes(lambda h: Qtok[0:P, h, :], 8, P, 96, qt[0:96, :, 0:P], [qtk], ["Qtok"], 0)
            transposes(lambda k: ckvb[0:P, k * 128:(k + 1) * 128], 2, P, 128, ckvT[:, :, 0:P], ["ckvT"], ["ckvb"], 0)
            kv_from_ckvT(P, kf, ("krf", cb_i), kt_dst, va_dst, kt_keys, va_keys, 0, 1)
            S.dma("pool", kv_dst, cf[0:P, :], [("ckvf", cb_i)], [], is_out=True)
            S.dma("pool", kr_dst, kf[0:P, :], [("krf", cb_i)], [], is_out=True)
            return s

        def kv_from_ckvT(P, kf, kfk, kt_dst, va_dst, kt_keys, va_keys, bk, bv):
            def fk(e):
                last = None
                for k in range(2):
                    e.matmul(psF[0:P, bk, :], lhsT=ckvT[:, k, 0:P], rhs=w_uk_t[:, k, :], start=(k == 0), stop=(k == 1))
                for k in range(2):
                    last = e.matmul(psF[0:P, bv, :], lhsT=ckvT[:, k, 0:P], rhs=w_uv_t[:, k, :], start=(k == 0), stop=(k == 1))
                return last
            S.op("pe", ["ckvT", "w_uk", "w_uv"], [("pf", bk), ("pf", bv)], fk)
            S.op("act", [], [("pf", bk), "Ktok"], lambda e: e.copy(out=Ktok[0:P, :, 0:64], in_=psF[0:P, bk, :].rearrange("p (h d) -> p h d", h=8)))
            S.op("dve", [kfk], ["Ktok"], lambda e: e.tensor_copy(out=Ktok[0:P, :, 64:96], in_=bc_ins(kf[0:P, :], 1, 8)))
            S.op("dve", [], [("pf", bv)] + list(va_keys), lambda e: e.tensor_copy(out=va_dst, in_=psF[0:P, bv, :].rearrange("p (h d) -> p h d", h=8)))
            transposes(lambda h: Ktok[0:P, h, :], 8, P, 96, kt_dst, kt_keys, ["Ktok"], 0)

        pt_i = [0]

        def attn_prompt(i, tb, qt, qtk):
            nk = i + 1
            glist = [(h, list(range(g, min(g + 4, nk)))) for h in range(8) for g in range(0, nk, 4)]

            def emit_s(idx):
                h, g = glist[idx]
                bb = idx % 2

                def fs(e):
                    last = None
                    for j, kt in enumerate(g):
                        last = e.matmul(psF[:, bb, j * 128:(j + 1) * 128], lhsT=KT[0:96, h, kt * 128:(kt + 1) * 128], rhs=qt[0:96, h, :], start=True, stop=True)
                    return last
                S.op("pe", [qtk] + [("KT", kt) for kt in g], [("pf", bb)], fs)
                n = len(g) * 128
                S.op("act", [], [("pf", bb), ("PT", bb)],
                     lambda e: e.activation(out=PT[bb][:].rearrange("p a b -> p (a b)")[:, 0:n], in_=psF[:, bb, 0:n], func=AF.Exp, scale=SCALE))
                if i in g:
                    j = g.index(i)
                    S.op("pool", [], [("PT", bb)], lambda e: e.memset(PT[bb][64:128, j, 0:64], 0.0))

            def emit_pv(idx):
                h, g = glist[idx]
                bb = idx % 2
                pob = 2 + h // 4
                pocol = (h % 4) * 128

                def fp(e):
                    last = None
                    for j, kt in enumerate(g):
                        last = e.matmul(psF[:, pob, pocol:pocol + 65], lhsT=PT[bb][:, j, :], rhs=VA[:, kt, h, :], start=(kt == 0), stop=(kt == nk - 1))
                    return last
                S.op("pe", [("PT", bb)] + [("VA", kt) for kt in g], [("pf", pob)], fp)

            for idx in range(len(glist)):
                emit_s(idx)
                if idx >= 1:
                    emit_pv(idx - 1)
            emit_pv(len(glist) - 1)
            pov = psF[:, 2:4, :].rearrange("p b (h d) -> p (b h) d", h=4)
            S.op("dve", [], [("pf", 2), ("pf", 3), "rec"], lambda e: e.reciprocal(out=rec[:].rearrange("p (h a) -> p h a", a=1), in_=pov[:, :, 64:65]))
            S.op("dve", ["rec"], [("pf", 2), ("pf", 3), "otok"],
                 lambda e: e.tensor_tensor(out=otok[:].rearrange("p (h d) -> p h d", h=8), in0=pov[:, :, 0:64], in1=bc_ins(rec[:], 2, 64), op=ALU.mult))
            so, sok = sumsq(otok[:], ["otok"], 128, 512)
            ro, rok = rstd(so, sok, 512)
            S.op("act", ["otok", rok], [("tokb", tb, 0)], lambda e: e.activation(out=tokb[tb][:, 0:512], in_=otok[:], func=AF.Copy, scale=ro[:]))


        def attn_sample(sq, tb, qt, qtk):
            NKT = PAST // 128
            first = [True, True]

            def st_load(kt):
                b = kt % 4
                S.dma("pool", cbuf[b], ckv_c[sq, kt * 128:(kt + 1) * 128, :], [], [("cbuf", b)])
                S.dma("sp", krc[b], ckr_c[sq, kt * 128:(kt + 1) * 128, :], [], [("krc", b)])

            def st_t1(kt):
                b, c2 = kt % 4, kt % 2
                transposes(lambda k: cbuf[b][:, k * 128:(k + 1) * 128], 2, 128, 128, ckvT2[c2], [("ckvT2", c2)], [("cbuf", b)], 0)

            def st_m(kt):
                b, c2 = kt % 4, kt % 2

                def fk(e):
                    last = None
                    for k in range(2):
                        e.matmul(psF[:, 0, :], lhsT=ckvT2[c2][:, k, :], rhs=w_uk_t[:, k, :], start=(k == 0), stop=(k == 1))
                    for k in range(2):
                        last = e.matmul(psF[:, 1, :], lhsT=ckvT2[c2][:, k, :], rhs=w_uv_t[:, k, :], start=(k == 0), stop=(k == 1))
                    return last
                S.op("pe", [("ckvT2", c2), "w_uk", "w_uv"], [("pf", 0), ("pf", 1)], fk)
                S.op("act", [], [("pf", 0), ("Ktok2", c2)], lambda e: e.copy(out=Ktok2[c2][:, :, 0:64], in_=psF[:, 0, :].rearrange("p (h d) -> p h d", h=8)))
                S.op("dve", [("krc", b)], [("Ktok2", c2)], lambda e: e.tensor_copy(out=Ktok2[c2][:, :, 64:96], in_=bc_ins(krc[b], 1, 8)))
                S.op("dve", [], [("pf", 1), ("VAs", b)], lambda e: e.tensor_copy(out=VAs[b][:, :, 0:64], in_=psF[:, 1, :].rearrange("p (h d) -> p h d", h=8)))

            def st_t2(kt):
                b, c2 = kt % 4, kt % 2
                transposes(lambda h: Ktok2[c2][:, h, :], 8, 128, 96, KTs[b][0:96, :, :], [("KTs", b)], [("Ktok2", c2)], 0)

            def bufs(kt):
                if kt == NKT:
                    return KTn[sq], ("KTn", sq), VAn[sq], ("VAn", sq), 16
                b = kt % 4
                return KTs[b], ("KTs", b), VAs[b], ("VAs", b), 128

            def st_b(kt):
                kt_t, kt_k, _, _, KP = bufs(kt)
                bb = kt % 2

                def fs(e):
                    last = None
                    for h in range(8):
                        last = e.matmul(psF[0:KP, bb, h * 16:(h + 1) * 16], lhsT=kt_t[0:96, h, 0:KP], rhs=qt[0:96, h, 0:16], start=True, stop=True)
                    return last
                S.op("pe", [qtk, kt_k], [("pf", bb)], fs)
                S.op("act", [], [("pf", bb), ("PT", bb)],
                     lambda e: e.activation(out=PT[bb][0:KP, 0, :], in_=psF[0:KP, bb, 0:128], func=AF.Exp, scale=SCALE))

            def st_c(kt):
                _, _, va_t, va_k, KP = bufs(kt)
                bb = kt % 2
                new = kt == NKT

                def fp(e):
                    last = None
                    for h in range(8):
                        st = first[h // 4]
                        first[h // 4] = False
                        last = e.matmul(psF[0:16, 2 + h // 4, (h % 4) * 128:(h % 4) * 128 + 65], lhsT=PT[bb][0:KP, 0, h * 16:(h + 1) * 16],
                                        rhs=va_t[0:KP, h, :], start=st, stop=new, skip_group_check=True)
                    return last
                S.op("pe", [("PT", bb), va_k], [("pf", 2), ("pf", 3)], fp)

            for step in range(NKT + 6):
                if 0 <= step - 5 <= NKT:
                    st_c(step - 5)
                if 0 <= step - 4 <= NKT:
                    st_b(step - 4)
                if 0 <= step - 3 < NKT:
                    st_t2(step - 3)
                if 0 <= step - 2 < NKT:
                    st_m(step - 2)
                if 0 <= step - 1 < NKT:
                    st_t1(step - 1)
                if step < NKT:
                    st_load(step)
            pov = psF[0:16, 2:4, :].rearrange("p b (h d) -> p (b h) d", h=4)
            S.op("dve", [], [("pf", 2), ("pf", 3), "rec"], lambda e: e.reciprocal(out=rec[0:16, :].rearrange("p (h a) -> p h a", a=1), in_=pov[:, :, 64:65]))
            S.op("dve", ["rec"], [("pf", 2), ("pf", 3), "otok"],
                 lambda e: e.tensor_tensor(out=otok[0:16, :].rearrange("p (h d) -> p h d", h=8), in0=pov[:, :, 0:64], in1=bc_ins(rec[0:16, :], 2, 64), op=ALU.mult))
            so, sok = sumsq(otok[0:16, :], ["otok"], 16, 512)
            ro, rok = rstd(so, sok, 512, 16)
            S.op("act", ["otok", rok], [("tokb", tb, 0)], lambda e: e.activation(out=tokb[tb][0:16, 0:512], in_=otok[0:16, :], func=AF.Copy, scale=ro[0:16]))

        wg_i = [0]
        wd_i = [0]
        NWG = 4

        def fm_chunk(wsrc_r, c, T, bank, rhsT, rkeys):
            bufs_, cnt_, kn = pools["wg"]
            wb = cnt_[0] % len(bufs_)
            cnt_[0] += 1
            S.dma("sp", wg[wb][:], wsrc_r[0][c * 128:(c + 1) * 128, :].rearrange("p (k j) -> p k j", k=8), S.wkeys[wsrc_r[1]], [("wg", wb)])

            def f(e):
                last = None
                for k in range(8):
                    last = e.matmul(psF[:, bank, 0:T], lhsT=wg[wb][:, k, :], rhs=rhsT[:, k, 0:T], start=(k == 0), stop=(k == 7))
                return last
            S.op("pe", [("wg", wb)] + rkeys, [("pf", bank)], f)
            return wb

        def fm_chunk_g(wsrc_r, c, T, bank, rhsT, rkeys, pool="wg"):
            bufs_, cnt_, kn = pools[pool]
            wb = cnt_[0] % len(bufs_)
            cnt_[0] += 1
            wt = bufs_[wb]
            S.dma("sp", wt, wsrc_r[0][c * 128:(c + 1) * 128, :].rearrange("p (k j) -> p k j", k=8), S.wkeys[wsrc_r[1]], [(kn, wb)])
            for k0 in (0, 4):
                def f(e, k0=k0):
                    last = None
                    for k in range(k0, k0 + 4):
                        last = e.matmul(psF[:, bank, 0:T], lhsT=wt[:, k, :], rhs=rhsT[:, k, 0:T], start=(k == 0), stop=(k == 7))
                    return last
                S.op("pe", [(kn, wb)] + rkeys, [("pf", bank)], f)
                yield False

        def tm_pass(wsrc, krows, half, lhs_fn, lhs_keys, tiles, banks, pool="wd"):
            nk = len(krows)
            bufs_, cnt_, kn = pools[pool]
            for kk, kr in enumerate(krows):
                b = cnt_[0] % len(bufs_)
                cnt_[0] += 1
                wt = bufs_[b]
                S.dma("sp", wt, wsrc[0][kr * 128:(kr + 1) * 128, half * 512:(half + 1) * 512], S.wkeys[wsrc[1]], [(kn, b)])

                def f(e, wt=wt, kk=kk):
                    last = None
                    for (P, c0), bank in zip(tiles, banks):
                        last = e.matmul(psF[0:P, bank, :], lhsT=lhs_fn(kk, P, c0), rhs=wt, start=(kk == 0), stop=(kk == nk - 1))
                    return last
                S.op("pe", [(kn, b)] + list(lhs_keys), [("pf", bank) for bank in banks[:len(tiles)]], f)
                yield False

        def resid_add(slot, P, bank, half):
            xt = xh[slot]
            S.op("dve", [], [("pf", bank), ("xh", slot)],
                 lambda e: e.tensor_tensor(out=xt[0:P, half * 512:(half + 1) * 512], in0=psF[0:P, bank, :], in1=xt[0:P, half * 512:(half + 1) * 512], op=ALU.add))

        def block_x(tiles, T, blk, is_prompt):
            akeys = [("actT", td["col0"]) for td in tiles]
            HM = 'fao'
            S.hook_on = 'f' in HM
            S.hook_rate = 3.0
            pre = [load_x(td["x_ap"], td["P"]) for td in tiles]
            slots = []
            for j, td in enumerate(tiles):
                slots.append(td["front"](pre[j]))
                if not late_done[0]:
                    late_done[0] = True
                    convert(conv_late)
            S.hook_rate = 2.0
            last30 = is_prompt and blk == 3
            use45 = last30 or not is_prompt
            if use45:
                S.hook_on = False
                S.drain_to_safe()
            if is_prompt and blk > 0:
                S.op("act", ["uTp"], ["uTp"], lambda e: e.copy(out=uTp[:, :, 0:30], in_=uTp[:, :, 512:542]))
            if not is_prompt:
                for sq in range(2):
                    S.dma("sp", utok[0:30, :], cst[sq], [], ["utok"])
                    S.dma("pool", cv_s[sq, 0:14, :], cst[sq, 16:30, :], [], [], is_out=True)
                    S.op("act", ["utok"], [("dwT", 0)], lambda e: e.copy(out=dwT[0:30, 0, :], in_=utok[0:30, :]))
                    transposes(lambda k: dwT[0:30, 0, k * 128:(k + 1) * 128], 4, 30, 128, uTs[:, :, sq, 0:30], ["uTs"], [("dwT", 0)], 1)

            def tokmajor_rows(c, wb, rows0, nrows):
                def ft(e):
                    last = None
                    for k in range(8):
                        last = e.matmul(psF[0:nrows, 2 + c // 4, (c % 4) * 128:(c % 4 + 1) * 128], lhsT=actT[:, k, rows0:rows0 + nrows],
                                        rhs=wg[wb][:, k, :], start=(k == 0), stop=(k == 7))
                    return last
                S.op("pe", [("wg", wb)] + akeys, [("pf", 2 + c // 4)], ft)

            for ch in range(4):
                ab, gb = ch % 2, (4 if use45 else 2) + ch % 2
                wba = fm_chunk((wcb, "wsc_c"), ch, T, ab, actT, akeys)
                if last30:
                    tokmajor_rows(ch, wba, 482, 30)
                if not is_prompt:
                    tokmajor_rows(ch, wba, 0, 32)
                wbg = fm_chunk((wcb, "wsc_c"), 4 + ch, T, gb, actT, akeys)
                if last30:
                    tokmajor_rows(4 + ch, wbg, 482, 30)
                if not is_prompt:
                    tokmajor_rows(4 + ch, wbg, 0, 32)
                gt, gk = (t1, "t1") if ch % 2 == 0 else (otok, "otok")
                S.op("act", [], [("pf", gb), gk], lambda e, gb=gb, gt=gt: e.activation(out=gt[:, 0:T], in_=psF[:, gb, 0:T], func=AF.Tanh, scale=0.5))
                S.op("dve", [gk], [("pf", ab), gk],
                     lambda e, ab=ab, gt=gt: e.scalar_tensor_tensor(out=gt[:, 0:T], in0=gt[:, 0:T], scalar=1.0, in1=psF[:, ab, 0:T], op0=ALU.add, op1=ALU.mult))
                if is_prompt:
                    S.op("act", [gk], ["uTp"], lambda e, ch=ch, gt=gt: e.activation(out=uTp[:, ch, 30:30 + T], in_=gt[:, 0:T], func=AF.Copy, scale=0.5))
                else:
                    S.op("act", [gk], ["uTs"], lambda e, ch=ch, gt=gt: e.activation(out=uTs[:, ch, :, 30:46], in_=gt[:, 0:32].rearrange("p (s t) -> p s t", s=2), func=AF.Copy, scale=0.5))
            nr = 30 if last30 else (32 if not is_prompt else 0)
            if nr:
                S.op("act", [], [("pf", 3), "t1"], lambda e: e.activation(out=t1[0:nr, :], in_=psF[0:nr, 3, :], func=AF.Tanh, scale=0.5))
                S.op("dve", ["t1"], [("pf", 2), "t2"],
                     lambda e: e.scalar_tensor_tensor(out=t2[0:nr, :], in0=t1[0:nr, :], scalar=1.0, in1=psF[0:nr, 2, :], op0=ALU.add, op1=ALU.mult))
                S.op("dve", ["t2"], ["utok"], lambda e: e.tensor_scalar(utok[0:nr, :], t2[0:nr, :], 0.5, None, ALU.mult))
                if last30:
                    S.dma("pool", cv_p, utok[0:30, :], ["utok"], [], is_out=True)
                else:
                    for sq in range(2):
                        S.dma("pool", cv_s[sq, 14:30, :], utok[sq * 16:(sq + 1) * 16, :], ["utok"], [], is_out=True)
            HALF = ((0, 16), (16, 31))
            for ch in range(4):
                wv = sp_t[:, C_WDW + ch * 31:C_WDW + (ch + 1) * 31]
                for hi, (j0, j1) in enumerate(HALF):
                    S.op("dve", ["sp_t", "ident"], [("diag", hi)],
                         lambda e, wv=wv, j0=j0, j1=j1: e.tensor_tensor(out=diag[:, j0:j1, :], in0=bc_ins(ident[:], 1, j1 - j0),
                                                                        in1=bc_ins(wv[:, j0:j1], 2, 128), op=ALU.mult))
                bank = (4 if use45 else 0) + ch % 2
                for hi, (j0, j1) in enumerate(HALF):
                    def fc(e, ch=ch, bank=bank, j0=j0, j1=j1):
                        last = None
                        if is_prompt:
                            for j in range(j0, j1):
                                last = e.matmul(psF[:, bank, 0:T], lhsT=diag[:, j, :], rhs=uTp[:, ch, j:j + T], start=(j == 0), stop=(j == 30))
                        else:
                            for sq in range(2):
                                for j in range(j0, j1):
                                    last = e.matmul(psF[:, bank, sq * 16:(sq + 1) * 16], lhsT=diag[:, j, :], rhs=uTs[:, ch, sq, j:j + 16],
                                                    start=(j == 0 and sq == 0), stop=(j == 30), skip_group_check=True)
                        return last
                    S.op("pe", [("diag", hi), "uTp", "uTs"], [("pf", bank)], fc)
                S.op("act", ["sp_t"], [("pf", bank), ("dwT", ch)],
                     lambda e, ch=ch, bank=bank: e.activation(out=dwT[:, ch, 0:T], in_=psF[:, bank, 0:T], func=AF.Identity, bias=sp_t[:, C_BDW + ch:C_BDW + ch + 1]))
            S.hook_on = 'a' in HM
            est_ops = sum(td["nops"] for td in tiles) + 16
            S.hook_rate = min(3.0, max(0.2, (S.hook_total - S.hook_steps) / float(est_ops)))
            def gen_convln(td, tb):
                P, c0 = td["P"], td["col0"]
                pk = ("pb", 1)

                def ftr(e):
                    last = None
                    for ch in range(4):
                        last = e.transpose(psB[0:P, 1, ch * 128:(ch + 1) * 128], dwT[:, ch, c0:c0 + P], ident[:, :])
                    return last
                S.op("pe", [("dwT", ch) for ch in range(4)] + ["ident"], [pk], ftr)
                s1, s1k = newstat()
                S.op("act", [], [pk, "t1", s1k], lambda e: e.activation(out=t1[0:P, :], in_=psB[0:P, 1, 0:512], func=AF.Copy, accum_out=s1[0:P]))
                yield
                s2, s2k = sumsq(t1[0:P, :], ["t1"], P, 512)
                mean, mk = newstat()
                msq, msk = newstat()
                ve, vek = newstat()
                rs_, rsk = newstat()
                S.op("dve", [s1k], [mk], lambda e: e.tensor_scalar(mean[0:P], s1[0:P], 1.0 / 512, None, ALU.mult))
                S.op("dve", [mk], [msk], lambda e: e.tensor_tensor(out=msq[0:P], in0=mean[0:P], in1=mean[0:P], op=ALU.mult))
                yield
                S.op("dve", [s2k, msk], [vek], lambda e: e.scalar_tensor_tensor(out=ve[0:P], in0=s2[0:P], scalar=1.0 / 512, in1=msq[0:P], op0=ALU.mult, op1=ALU.subtract))
                S.op("dve", [vek], [vek], lambda e: e.tensor_scalar(ve[0:P], ve[0:P], EPS, None, ALU.add))
                S.op("pool", [vek, "mhalf"], [rsk], lambda e: e.tensor_tensor(rs_[0:P], ve[0:P], mhalf[0:P], ALU.pow))
                yield
                S.op("dve", [mk, rsk, "t1"], ["t1"], lambda e: e.tensor_scalar(t1[0:P, :], t1[0:P, :], mean[0:P], rs_[0:P], ALU.subtract, ALU.mult))
                yield
                S.op("dve", ["t1", "bc_t"], ["t1"], lambda e: e.tensor_tensor(out=t1[0:P, :], in0=t1[0:P, :], in1=bc_t[0:P, B_GCN:B_GCN + 512], op=ALU.mult))
                yield
                S.op("dve", ["t1", "bc_t"], ["t1"], lambda e: e.tensor_tensor(out=t1[0:P, :], in0=t1[0:P, :], in1=bc_t[0:P, B_BCN:B_BCN + 512], op=ALU.add))
                yield
                S.op("act", ["t1"], ["t2"], lambda e: e.activation(out=t2[0:P, :], in_=t1[0:P, :], func=AF.Tanh, scale=0.5))
                yield
                S.op("dve", ["t1", "t2"], ["t2"], lambda e: e.scalar_tensor_tensor(out=t2[0:P, :], in0=t2[0:P, :], scalar=1.0, in1=t1[0:P, :], op0=ALU.add, op1=ALU.mult))
                yield
                sc, sck = sumsq(t2[0:P, :], ["t2"], P, 512, scale=0.5)
                rc, rck = rstd(sc, sck, 512, P)
                yield
                S.op("act", ["t2", rck], [("tokb", tb, 1)], lambda e: e.activation(out=tokb[tb][0:P, 512:1024], in_=t2[0:P, :], func=AF.Copy, scale=rc[0:P]))
                yield

            tbs = []
            for td in tiles:
                tbs.append(tokb_i[0] % 2)
                tokb_i[0] += 1
            S.run_side(gen_convln(tiles[0], tbs[0]))
            for j, td in enumerate(tiles):
                P, c0 = td["P"], td["col0"]
                tb = tbs[j]
                S.hook2 = gen_convln(tiles[j + 1], tbs[j + 1]) if j + 1 < len(tiles) else None
                td["attn"](tb)
                if S.hook2 is not None:
                    S.run_side(S.hook2)
                    S.hook2 = None
                transposes(lambda k, tb=tb, P=P: tokb[tb][0:P, k * 128:(k + 1) * 128], 8, P, 128, actT[:, :, c0:c0 + P], [("actT", c0)],
                           [("tokb", tb, 0), ("tokb", tb, 1)], 0, gain=sp_t[:, C_GMIX:C_GMIX + 8])
            tl = [(td["P"], td["col0"]) for td in tiles]
            S.hook_on = 'o' in HM
            for half in range(2):
                for _ in tm_pass((wob, "wsc_o"), list(range(8)), half, lambda k, P, c0: actT[:, k, c0:c0 + P], akeys, tl, [0, 1, 2, 3]):
                    pass
                for t, (P, c0) in enumerate(tl):
                    resid_add(slots[t], P, t, half)
            S.hook_on = False
            return slots

        def gen_ffn(tiles, slots, T, deep=False):
            gp, dp = ("wgS", "wdS") if deep else ("wg", "wd")
            fkeys = [("fT", td["col0"]) for td in tiles]
            tl = [(td["P"], td["col0"]) for td in tiles]
            for t, td in enumerate(tiles):
                P, c0 = td["P"], td["col0"]
                xt = xh[slots[t]]
                sf, sfk = sumsq(xt[0:P, :], [("xh", slots[t])], P, D)
                rf, rfk = rstd(sf, sfk, D, P)
                S.op("act", [("xh", slots[t]), rfk], ["fbuf"],
                     lambda e, P=P, xt=xt, rf=rf: e.activation(out=fbuf[0:P, :], in_=xt[0:P, :], func=AF.Copy, scale=rf[0:P]))
                transposes(lambda k, P=P: fbuf[0:P, k * 128:(k + 1) * 128], 8, P, 128, fT[:, :, c0:c0 + P], [("fT", c0)],
                           ["fbuf"], 1, gain=sp_t[:, C_LNFFN:C_LNFFN + 8])
                yield True
            pairs = [list(range(i, min(i + 2, len(tiles)))) for i in range(0, len(tiles), 2)]
            for r in range(2):
                for cc in range(11):
                    c = r * 11 + cc
                    yield from fm_chunk_g((wgb, "wsc_g"), c, T, 4, fT, fkeys, gp)
                    yield from fm_chunk_g((wub, "wsc_u"), c, T, 5, fT, fkeys, gp)
                    tmp, tk = (t4[:, 0:T], "t4") if cc % 2 == 0 else (fbuf[:].bitcast(F32)[:, 0:T], "fbuf")
                    S.op("act", [], [("pf", 4), tk], lambda e, tmp=tmp: e.activation(out=tmp, in_=psF[:, 4, 0:T], func=AF.Tanh, scale=0.5))
                    S.op("dve", [tk], [("pf", 4), tk],
                         lambda e, tmp=tmp: e.scalar_tensor_tensor(out=tmp, in0=tmp, scalar=1.0, in1=psF[:, 4, 0:T], op0=ALU.add, op1=ALU.mult))
                    S.op("dve", [tk], [("pf", 5), ("aT", cc)],
                         lambda e, cc=cc, tmp=tmp: e.scalar_tensor_tensor(out=aT[:, cc, 0:T], in0=tmp, scalar=0.5, in1=psF[:, 5, 0:T], op0=ALU.mult, op1=ALU.mult))
                    yield True
                for pr in pairs:
                    ptl = [tl[t] for t in pr]
                    for half in range(2):
                        yield from tm_pass((wdb, "wsc_d"), [r * 11 + k for k in range(11)], half, lambda k, P, c0: aT[:, k, c0:c0 + P],
                                           [("aT", k) for k in range(11)], ptl, [4, 5], dp)
                        for j, t in enumerate(pr):
                            resid_add(slots[t], ptl[j][0], 4 + j, half)
                        yield True
            for t, td in enumerate(tiles):
                P = td["P"]
                xt = xh[slots[t]]
                sy, syk = sumsq(xt[0:P, :], [("xh", slots[t])], P, D)
                ry, ryk = rstd(sy, syk, D, P)
                S.op("dve", [ryk, "bc_t"], [("xh", slots[t])],
                     lambda e, P=P, xt=xt, ry=ry: e.scalar_tensor_tensor(out=xt[0:P, :], in0=xt[0:P, :], scalar=ry[0:P], in1=bc_t[0:P, B_GFIN:B_GFIN + D], op0=ALU.mult, op1=ALU.mult))
                S.dma("pool", td["y_dst"], xt[0:P, :], [("xh", slots[t])], [], is_out=True)
                yield True

        QTs = [sb(f"QT{i}", [128, 8, 128], BF16) for i in range(4)]
        print("SBUF bytes remaining", nc.sbuf_bytes_remaining)

        blocks = []
        nblk = 4 if not dbg else (dbg if dbg > 0 else 0)
        for blk in range(nblk):
            tiles = []
            for j in range(4):
                i = blk * 4 + j
                td = dict(P=128, col0=j * 128, y_dst=y_p[i * 128:(i + 1) * 128, :])
                td["x_ap"] = xp[i * 128:(i + 1) * 128, :]
                td["front"] = (lambda slot, i=i, j=j: front(slot, 128, j * 128, i,
                                                      kv_p[i * 128:(i + 1) * 128, :], kr_p[i * 128:(i + 1) * 128, :],
                                                      KT[0:96, :, i * 128:(i + 1) * 128], VA[:, i, :, 0:64], [("KT", i)], [("VA", i)],
                                                      QTs[j], ("QT", j)))
                td["attn"] = (lambda tb, i=i, j=j: attn_prompt(i, tb, QTs[j], ("QT", j)))
                td["nops"] = 2 + 16 * ((i + 4) // 4)
                tiles.append(td)
            blocks.append((tiles, 512, blk, True))
        if do_sample:
            tiles = []
            for sq in range(2):
                td = dict(P=16, col0=sq * 16, y_dst=y_s[sq * 16:(sq + 1) * 16, :])
                td["x_ap"] = xs[sq * 16:(sq + 1) * 16, :]
                td["front"] = (lambda slot, sq=sq: front(slot, 16, sq * 16, 16,
                                                   kv_s[sq * 16:(sq + 1) * 16, :], kr_s[sq * 16:(sq + 1) * 16, :],
                                                   KTn[sq][0:96, :, 0:16], VAn[sq][0:16, :, 0:64], [("KTn", sq)], [("VAn", sq)],
                                                   QTs[sq], ("QT", sq)))
                td["attn"] = (lambda tb, sq=sq: attn_sample(sq, tb, QTs[sq], ("QT", sq)))
                td["nops"] = 2 + 33 * 6
                tiles.append(td)
            blocks.append((tiles, 32, 4, False))

        samp_keys = [(n, i) for n in ("KTs", "VAs", "cbuf", "krc") for i in range(4)] + \
                    [(n, i) for n in ("KTn", "VAn", "ckvT2", "Ktok2") for i in range(2)] + ["uTs"]
        pending = None
        pending_units = 0
        for (tiles, T, blk, is_prompt) in blocks:
            if not is_prompt:
                S.op("pool", [], samp_keys + [("KT", i) for i in range(NTILE)], lambda e: e.memset(scr[:], 0.0))
                for i in range(4):
                    S.op("pool", [], [("VAs", i)], lambda e, i=i: e.memset(VAs[i], 1.0))
                for i in range(2):
                    S.op("pool", [], [("VAn", i)], lambda e, i=i: e.memset(VAn[i], 1.0))
            S.hook = pending
            S.hook_steps = 0
            S.hook_acc = 0.0
            S.hook_total = pending_units
            slots = block_x(tiles, T, blk, is_prompt)
            S.hook = None
            if pending is not None:
                for _ in pending:
                    pass
            last_blk = (tiles is blocks[-1][0]) and not is_prompt
            if last_blk:
                S.op("pool", [], samp_keys + [("wgS", i) for i in range(len(wgS))] + [("wdS", i) for i in range(len(wdS))],
                     lambda e: e.memset(scr[:], 0.0))
            pending = gen_ffn(tiles, slots, T, deep=last_blk)
            npairs = (len(tiles) + 1) // 2
            pending_units = 2 * len(tiles) + NFF * 5 + 2 * npairs * 2 * 12
        if pending is not None:
            for _ in pending:
                pass
        S.finish()
    return nc


def _consts():
    ident = np.eye(128, dtype=np.float32).astype(ml_dtypes.bfloat16)
    inv = (1.0 / (np.float32(10000.0) ** (np.arange(0, 32, 2, dtype=np.float32) / np.float32(32)))).astype(np.float32)
    pos = np.zeros((128, 17), np.float32)
    for i in range(16):
        pos[:, i] = i * 128 + np.arange(128)
    pos[:, 16] = PAST + (np.arange(128) % 16)
    return ident, inv, pos


def _tile_cols(w):
    C = w.shape[1]
    return np.ascontiguousarray(w.reshape(8, 128, C // 128, 128).transpose(2, 1, 0, 3))


def kernel(x_prompt, x_sample, cache_kv_latent, cache_k_rope, state_conv,
           ln_mix, w_in, g_q, w_uq, g_kv, w_uk, w_uv, w_dw, b_dw, g_cn, b_cn, g_om, g_oc,
           w_out, ln_ffn, w_gate, w_up, w_down, g_final, _dbg=False):
    f = lambda a: np.ascontiguousarray(np.asarray(a, dtype=np.float32))
    ident, inv, pos = _consts()
    smallp = np.zeros((128, C_END), np.float32)
    smallp[:, C_LNMIX:C_LNMIX + 8] = f(ln_mix)[0].reshape(8, 128).T
    smallp[:, C_GQ:C_GQ + 3] = f(g_q)[0].reshape(3, 128).T
    smallp[:, C_GMIX:C_GMIX + 4] = f(g_om)[0].reshape(4, 128).T
    smallp[:, C_GMIX + 4:C_GMIX + 8] = f(g_oc)[0].reshape(4, 128).T
    smallp[:, C_LNFFN:C_LNFFN + 8] = f(ln_ffn)[0].reshape(8, 128).T
    smallp[:, C_BDW:C_BDW + 4] = f(b_dw)[0].reshape(4, 128).T
    smallp[:, C_WDW:C_WDW + 124] = f(w_dw)[0].reshape(31, 4, 128).transpose(2, 1, 0).reshape(128, 124)
    bcp = np.zeros((1, B_END), np.float32)
    bcp[0, B_GKV:B_GKV + 256] = f(g_kv)[0]
    bcp[0, B_GCN:B_GCN + 512] = f(g_cn)[0]
    bcp[0, B_BCN:B_BCN + 512] = f(b_cn)[0]
    bcp[0, B_GFIN:B_GFIN + 1024] = f(g_final)
    bcp[0, B_INV:B_INV + 16] = inv
    w_in0 = f(w_in)[0]
    shared = dict(
        w_in=np.ascontiguousarray(w_in0[:, 0:672]), wc_r=_tile_cols(w_in0[:, 672:1696]),
        w_uq=f(w_uq)[0], w_uk=f(w_uk)[0].reshape(256, 512), w_uv=f(w_uv)[0].reshape(256, 512),
        w_out=f(w_out)[0], wg_r=_tile_cols(f(w_gate)[0]), wu_r=_tile_cols(f(w_up)[0]), w_down=f(w_down)[0],
        smallp=smallp, bcp=bcp, postab=pos, identd=ident)
    xp_, xs_ = f(x_prompt), f(x_sample)
    ckv_, ckr_, cst_ = f(cache_kv_latent)[0], f(cache_k_rope)[0], f(state_conv)[0]
    in_maps = []
    for c in range(N_CORES):
        m = dict(shared)
        m["xp"] = xp_[c]
        m["xs"] = xs_[2 * c:2 * c + 2].reshape(32, D)
        m["ckv_c"] = ckv_[2 * c:2 * c + 2]
        m["ckr_c"] = ckr_[2 * c:2 * c + 2]
        m["cst"] = cst_[2 * c:2 * c + 2]
        in_maps.append(m)
    nc = build_program(dbg=_dbg)
    res = run_bass_kernel_spmd(nc, in_maps, core_ids=list(range(N_CORES)))
    R = res.results
    y_prompt = np.stack([R[c]["y_p"] for c in range(N_CORES)])
    y_sample = np.concatenate([R[c]["y_s"].reshape(2, 16, D) for c in range(N_CORES)])
    kvp = np.stack([R[c]["kv_p"] for c in range(N_CORES)])[None]
    krp = np.stack([R[c]["kr_p"] for c in range(N_CORES)])[None]
    cvp = np.stack([R[c]["cv_p"] for c in range(N_CORES)])[None]
    kvs = np.concatenate([R[c]["kv_s"].reshape(2, 16, 256) for c in range(N_CORES)])[None]
    krs = np.concatenate([R[c]["kr_s"].reshape(2, 16, 32) for c in range(N_CORES)])[None]
    cvs = np.concatenate([R[c]["cv_s"] for c in range(N_CORES)])[None]
    return tuple(np.ascontiguousarray(a.astype(np.float32)) for a in (y_prompt, y_sample, kvp, krp, cvp, kvs, krs, cvs))
```
